# Optimizing a Trainium2 kernel written in Bass

```python
import math
import jax, jax.numpy as jnp
from jax import lax
import numpy as np

D_MODEL = 2048
BATCH = 16
SEQ = 256
DEPTH = 4
DEC_BATCH = 8
DEC_SEQ = 1024
PAST_LEN = 256

GRID_W = 64
HEAD_DIM = 128
GROUP_WIDTH = D_MODEL // 4
NA_HEADS = GROUP_WIDTH // HEAD_DIM
NA_WIN_ROWS = 8
NA_WIN_COLS = 16
GLA_HEADS = 4
GLA_DV = GROUP_WIDTH // GLA_HEADS
GLA_DK = GLA_DV // 2
GLA_GATE_RANK = 16
GLA_TAU = 16.0
GQA_HEADS = GROUP_WIDTH // HEAD_DIM
GQA_KV_HEADS = GQA_HEADS // 2
GDN_HEADS = GROUP_WIDTH // HEAD_DIM
GDN_DK = HEAD_DIM
GDN_DV = HEAD_DIM
CONV_K = 5
CHUNK = 64
Q_BLOCK = 128
ROPE_THETA = 10000.0
FFN_DIM = ((8 * D_MODEL // 3 + 127) // 128) * 128
N_MOD = 9
EPS = 1e-6
NEG_INF = -1e30

IN_SPLITS = (
    ('na_q', NA_HEADS * HEAD_DIM), ('na_k', NA_HEADS * HEAD_DIM), ('na_v', NA_HEADS * HEAD_DIM),
    ('gla_q', GLA_HEADS * GLA_DK), ('gla_k', GLA_HEADS * GLA_DK), ('gla_v', GLA_HEADS * GLA_DV),
    ('gla_r', GLA_HEADS * GLA_DV), ('gla_gf', GLA_GATE_RANK), ('gla_gb', GLA_GATE_RANK),
    ('gqa_q', GQA_HEADS * HEAD_DIM), ('gqa_k', GQA_KV_HEADS * HEAD_DIM), ('gqa_v', GQA_KV_HEADS * HEAD_DIM),
    ('gdn_qkv', 3 * GDN_HEADS * HEAD_DIM), ('gdn_z', GDN_HEADS * GDN_DV),
    ('gdn_b', 2 * GDN_HEADS), ('gdn_a', 2 * GDN_HEADS),
)
IN_COLS = sum(width for _, width in IN_SPLITS)

kernel_name = 'hybrid_diffusion_parallel_heads_step'


def rms_norm(x, g):
    xf = x.astype(jnp.float32)
    y = xf * lax.rsqrt(jnp.mean(xf * xf, axis=-1, keepdims=True) + EPS)
    return (y * g.astype(jnp.float32)).astype(x.dtype)


def l2_norm(x):
    xf = x.astype(jnp.float32)
    return xf * lax.rsqrt(jnp.sum(xf * xf, axis=-1, keepdims=True) + EPS)


def split_cols(z):
    out, off = {}, 0
    for name, width in IN_SPLITS:
        out[name] = z[..., off:off + width]
        off += width
    return out


def to_heads(x, n):
    B, T, _ = x.shape
    return x.reshape(B, T, n, -1).transpose(0, 2, 1, 3)


def from_heads(x):
    B, n, T, d = x.shape
    return x.transpose(0, 2, 1, 3).reshape(B, T, n * d)


def flip_t(x):
    return x[:, :, ::-1]


def adaln(cond, w_mod, b_mod):
    m = jax.nn.silu(cond) @ w_mod + b_mod
    return jnp.split(m[:, None, :], N_MOD, axis=-1)


def swiglu(x, w_gu, w_down):
    g, u = jnp.split(x @ w_gu, 2, axis=-1)
    return (jax.nn.silu(g) * u) @ w_down


def axial_angles(T):
    t = jnp.arange(T)
    row = (t // GRID_W).astype(jnp.float32)
    col = (t % GRID_W).astype(jnp.float32)
    half = HEAD_DIM // 2
    inv = ROPE_THETA ** (-jnp.arange(0, half, 2, dtype=jnp.float32) / half)
    return row[:, None] * inv[None, :], col[:, None] * inv[None, :]


def rope_half(x, ang):
    x1, x2 = jnp.split(x, 2, axis=-1)
    cos, sin = jnp.cos(ang).astype(x.dtype), jnp.sin(ang).astype(x.dtype)
    return jnp.concatenate([x1 * cos - x2 * sin, x1 * sin + x2 * cos], axis=-1)


def apply_axial_rope(x, ang_r, ang_c):
    xr, xc = jnp.split(x, 2, axis=-1)
    return jnp.concatenate([rope_half(xr, ang_r), rope_half(xc, ang_c)], axis=-1)


def dense_attention(q, k, v):
    B, Hk, G, T, d = q.shape
    nb = T // Q_BLOCK
    qb = q.reshape(B, Hk, G, nb, Q_BLOCK, d).transpose(3, 0, 1, 2, 4, 5)
    scale = d ** -0.5

    def one_block(qi):
        s = jnp.einsum('bhgqd,bhkd->bhgqk', qi, k).astype(jnp.float32) * scale
        p = jax.nn.softmax(s, axis=-1).astype(v.dtype)
        return jnp.einsum('bhgqk,bhkd->bhgqd', p, v)

    o = lax.map(one_block, qb)
    return o.transpose(1, 2, 3, 0, 4, 5).reshape(B, Hk, G, T, d)


def neighbourhood_attention(q, k, v, ck, cv, rpb):
    B, H, T, d = q.shape
    rows = T // GRID_W
    wr = min(NA_WIN_ROWS, rows)
    r = jnp.arange(rows)
    krow = jnp.clip(r - wr // 2, 0, rows - wr)[:, None] + jnp.arange(wr)[None, :]
    col = jnp.arange(GRID_W)
    c0 = jnp.clip(col - NA_WIN_COLS // 2, 0, GRID_W - NA_WIN_COLS)
    in_win = (col[None, :] >= c0[:, None]) & (col[None, :] < c0[:, None] + NA_WIN_COLS)
    dr = krow - r[:, None] + (NA_WIN_ROWS - 1)
    dc = jnp.clip(col[None, :] - col[:, None] + (NA_WIN_COLS - 1), 0, 2 * NA_WIN_COLS - 2)
    bias = rpb[:, dr[:, None, :, None], dc[None, :, None, :]].astype(jnp.float32)
    qg = q.reshape(B, H, rows, GRID_W, d)
    kg = k.reshape(B, H, rows, GRID_W, d)[:, :, krow]
    vg = v.reshape(B, H, rows, GRID_W, d)[:, :, krow]
    scale = d ** -0.5
    s_loc = jnp.einsum('bhrqd,bhrjkd->bhrqjk', qg, kg).astype(jnp.float32) * scale + bias[None]
    s_loc = jnp.where(in_win[:, None, :], s_loc, NEG_INF)
    s_ctx = jnp.einsum('bhrqd,bhld->bhrql', qg, ck).astype(jnp.float32) * scale
    L = ck.shape[2]
    s = jnp.concatenate([s_ctx, s_loc.reshape(B, H, rows, GRID_W, wr * GRID_W)], axis=-1)
    p = jax.nn.softmax(s, axis=-1).astype(v.dtype)
    o = (jnp.einsum('bhrql,bhld->bhrqd', p[..., :L], cv)
         + jnp.einsum('bhrqjk,bhrjkd->bhrqd', p[..., L:].reshape(B, H, rows, GRID_W, wr, GRID_W), vg))
    return o.reshape(B, H, T, d)


def to_chunks(x):
    B, H, T = x.shape[:3]
    y = x.astype(jnp.float32).reshape((B, H, T // CHUNK, CHUNK) + x.shape[3:])
    return jnp.moveaxis(y, 2, 0)


def from_chunks(o):
    n, B, H, C, e = o.shape
    return jnp.moveaxis(o, 0, 2).reshape(B, H, n * C, e)


def gla_scan(q, k, v, log_a, s0):
    causal = jnp.tril(jnp.ones((CHUNK, CHUNK), bool))

    def step(S, inp):
        qi, ki, vi, gi = inp
        b = jnp.cumsum(gi, axis=2)
        diff = b[:, :, :, None, :] - b[:, :, None, :, :]
        decay = jnp.where(causal[:, :, None], jnp.exp(jnp.minimum(diff, 0.0)), 0.0)
        att = jnp.einsum('bhqd,bhkd,bhqkd->bhqk', qi, ki, decay)
        o = jnp.einsum('bhqd,bhde->bhqe', qi * jnp.exp(b), S) + jnp.einsum('bhqk,bhke->bhqe', att, vi)
        b_last = b[:, :, -1:, :]
        S = S * jnp.exp(b_last[:, :, 0, :, None]) + jnp.einsum('bhkd,bhke->bhde', ki * jnp.exp(b_last - b), vi)
        return S, o

    S, o = lax.scan(step, s0.astype(jnp.float32), (to_chunks(q), to_chunks(k), to_chunks(v), to_chunks(log_a)))
    return from_chunks(o), S


def gdn_scan(q, k, v, g, beta, s0):
    dv = v.shape[-1]
    causal = jnp.tril(jnp.ones((CHUNK, CHUNK), bool))
    strict = jnp.tril(jnp.ones((CHUNK, CHUNK), bool), -1)
    eye = jnp.eye(CHUNK, dtype=jnp.float32)

    def step(S, inp):
        qi, ki, vi, gi, bi = inp
        b = jnp.cumsum(gi, axis=-1)
        diff = b[..., :, None] - b[..., None, :]
        decay = jnp.where(causal, jnp.exp(jnp.minimum(diff, 0.0)), 0.0)
        kb = ki * bi[..., None]
        low = jnp.where(strict, jnp.einsum('bhqd,bhkd->bhqk', kb, ki) * decay, 0.0)
        rhs = jnp.concatenate([vi * bi[..., None], kb * jnp.exp(b)[..., None]], axis=-1)
        sol = lax.linalg.triangular_solve(low + eye, rhs, left_side=True, lower=True, unit_diagonal=True)
        u, w = sol[..., :dv], sol[..., dv:]
        v_new = u - w @ S
        att = jnp.einsum('bhqd,bhkd->bhqk', qi, ki) * decay
        o = (qi * jnp.exp(b)[..., None]) @ S + att @ v_new
        b_last = b[..., -1:]
        S = S * jnp.exp(b_last)[..., None] + jnp.einsum('bhkd,bhke->bhde', ki * jnp.exp(b_last - b)[..., None], v_new)
        return S, o

    xs = (to_chunks(q), to_chunks(k), to_chunks(v), to_chunks(g), to_chunks(beta))
    S, o = lax.scan(step, s0.astype(jnp.float32), xs)
    return from_chunks(o), S


def centred_depthwise_conv(x, w):
    C = x.shape[-1]
    pad = CONV_K // 2
    return lax.conv_general_dilated(x, w[:, None, :].astype(x.dtype), window_strides=(1,),
                                    padding=((pad, pad),), dimension_numbers=('NWC', 'WIO', 'NWC'),
                                    feature_group_count=C)


def token_mixing(h, P, l, ctx):
    B, T, _ = h.shape
    f32 = jnp.float32
    is_ctx = ctx is None
    z = split_cols(h @ P['w_in'][l])

    qn = rms_norm(to_heads(z['na_q'], NA_HEADS), P['na_qk_norm'][l, 0])
    kn = rms_norm(to_heads(z['na_k'], NA_HEADS), P['na_qk_norm'][l, 1])
    vn = to_heads(z['na_v'], NA_HEADS)
    if is_ctx:
        o_na = dense_attention(qn[:, :, None], kn, vn)[:, :, 0]
    else:
        o_na = neighbourhood_attention(qn, kn, vn, ctx['na_k'], ctx['na_v'], P['na_rpb'][l])

    qg = to_heads(z['gla_q'], GLA_HEADS) * (GLA_DK ** -0.5)
    kg = to_heads(z['gla_k'], GLA_HEADS)
    vg = to_heads(z['gla_v'], GLA_HEADS)
    la_f = to_heads(jax.nn.log_sigmoid((z['gla_gf'] @ P['gla_gate_up'][l, 0] + P['gla_gate_bias'][l, 0]).astype(f32)) / GLA_TAU, GLA_HEADS)
    la_b = to_heads(jax.nn.log_sigmoid((z['gla_gb'] @ P['gla_gate_up'][l, 1] + P['gla_gate_bias'][l, 1]).astype(f32)) / GLA_TAU, GLA_HEADS)
    if is_ctx:
        s0f = s0b = jnp.zeros((B, GLA_HEADS, GLA_DK, GLA_DV), f32)
    else:
        s0f, s0b = ctx['gla'][:, 0], ctx['gla'][:, 1]
    o_f, sf_gla = gla_scan(qg, kg, vg, la_f, s0f)
    o_b, sb_gla = gla_scan(flip_t(qg), flip_t(kg), flip_t(vg), flip_t(la_b), s0b)
    o_gla = from_heads(rms_norm(o_f + flip_t(o_b), P['gla_out_norm'][l])) * jax.nn.silu(z['gla_r'].astype(f32))

    qa = rms_norm(to_heads(z['gqa_q'], GQA_HEADS), P['gqa_qk_norm'][l, 0])
    ka = rms_norm(to_heads(z['gqa_k'], GQA_KV_HEADS), P['gqa_qk_norm'][l, 1])
    va = to_heads(z['gqa_v'], GQA_KV_HEADS)
    if is_ctx:
        keys, vals = ka, va
    else:
        ang_r, ang_c = axial_angles(T)
        qa = apply_axial_rope(qa, ang_r, ang_c)
        keys = jnp.concatenate([ctx['gqa_k'].astype(ka.dtype), apply_axial_rope(ka, ang_r, ang_c)], axis=2)
        vals = jnp.concatenate([ctx['gqa_v'].astype(va.dtype), va], axis=2)
    o_a = dense_attention(qa.reshape(B, GQA_KV_HEADS, GQA_HEADS // GQA_KV_HEADS, T, HEAD_DIM), keys, vals)
    o_gqa = from_heads(o_a.reshape(B, GQA_HEADS, T, HEAD_DIM))

    qkv = jax.nn.silu(centred_depthwise_conv(z['gdn_qkv'], P['gdn_conv'][l]))
    qd, kd, vd = jnp.split(qkv, 3, axis=-1)
    qd = l2_norm(to_heads(qd, GDN_HEADS)) * (GDN_DK ** -0.5)
    kd = l2_norm(to_heads(kd, GDN_HEADS))
    vd = to_heads(vd, GDN_HEADS)
    beta = jax.nn.sigmoid(z['gdn_b'].astype(f32)).reshape(B, T, 2, GDN_HEADS).transpose(2, 0, 3, 1)
    a_in = z['gdn_a'].astype(f32).reshape(B, T, 2, GDN_HEADS).transpose(2, 0, 3, 1)
    a_log = P['gdn_a_log'][l].astype(f32)
    dt_b = P['gdn_dt_bias'][l].astype(f32)
    g_f = -jnp.exp(a_log[0])[None, :, None] * jax.nn.softplus(a_in[0] + dt_b[0][None, :, None])
    g_b = -jnp.exp(a_log[1])[None, :, None] * jax.nn.softplus(a_in[1] + dt_b[1][None, :, None])
    if is_ctx:
        d0f = d0b = jnp.zeros((B, GDN_HEADS, GDN_DK, GDN_DV), f32)
    else:
        d0f, d0b = ctx['gdn'][:, 0], ctx['gdn'][:, 1]
    od_f, sf_gdn = gdn_scan(qd, kd, vd, g_f, beta[0], d0f)
    od_b, sb_gdn = gdn_scan(flip_t(qd), flip_t(kd), flip_t(vd), flip_t(g_b), flip_t(beta[1]), d0b)
    o_gdn = from_heads(rms_norm(od_f + flip_t(od_b), P['gdn_out_norm'][l])) * jax.nn.silu(z['gdn_z'].astype(f32))

    merged = jnp.concatenate([from_heads(o_na), o_gla.astype(h.dtype), o_gqa, o_gdn.astype(h.dtype)], axis=-1)
    out = merged @ P['w_out'][l]
    if is_ctx:
        return out, (kn, vn, ka, va, jnp.stack([sf_gla, sb_gla], axis=1), jnp.stack([sf_gdn, sb_gdn], axis=1))
    return out, None


def trunk_layer(x, cond, P, l, ctx):
    sh1, sc1, g1, sh2, sc2, g2, sh3, sc3, g3 = adaln(cond, P['w_mod'][l], P['b_mod'][l])
    h = rms_norm(x, P['norm_g'][l, 0]) * (1 + sc1) + sh1
    x = x + 0.5 * g1 * swiglu(h, P['ffn_gu'][l, 0], P['ffn_down'][l, 0])
    h = rms_norm(x, P['norm_g'][l, 1]) * (1 + sc2) + sh2
    m, cache = token_mixing(h, P, l, ctx)
    x = x + g2 * m
    h = rms_norm(x, P['norm_g'][l, 2]) * (1 + sc3) + sh3
    x = x + 0.5 * g3 * swiglu(h, P['ffn_gu'][l, 1], P['ffn_down'][l, 1])
    return x, cache


def setup_inputs(seed: int = 0) -> dict:
    key = jax.random.key(seed)
    ks = jax.random.split(key, 28)
    D = D_MODEL

    def nrm(k, shape, s):
        return jax.random.normal(k, shape, jnp.float32) * s

    dt = jnp.exp(jax.random.uniform(ks[25], (DEPTH, 2, GDN_HEADS), jnp.float32, math.log(1e-3), math.log(1e-1)))
    return {
        'x_prompt': nrm(ks[0], (BATCH, SEQ, D), 1.0),
        'x_sample': nrm(ks[1], (DEC_BATCH, DEC_SEQ, D), 1.0),
        'cache_na_k': nrm(ks[2], (DEC_BATCH, DEPTH, NA_HEADS, PAST_LEN, HEAD_DIM), 1.0),
        'cache_na_v': nrm(ks[3], (DEC_BATCH, DEPTH, NA_HEADS, PAST_LEN, HEAD_DIM), 1.0),
        'cache_gqa_k': nrm(ks[4], (DEC_BATCH, DEPTH, GQA_KV_HEADS, PAST_LEN, HEAD_DIM), 1.0),
        'cache_gqa_v': nrm(ks[5], (DEC_BATCH, DEPTH, GQA_KV_HEADS, PAST_LEN, HEAD_DIM), 1.0),
        'state_gla': nrm(ks[6], (DEC_BATCH, DEPTH, 2, GLA_HEADS, GLA_DK, GLA_DV), 0.5),
        'state_gdn': nrm(ks[7], (DEC_BATCH, DEPTH, 2, GDN_HEADS, GDN_DK, GDN_DV), 0.5),
        'c': nrm(ks[8], (DEC_BATCH, D), 1.0),
        'c_ctx': nrm(ks[9], (D,), 1.0),
        'norm_g': 1.0 + nrm(ks[10], (DEPTH, 3, D), 0.02),
        'w_mod': nrm(ks[11], (DEPTH, D, N_MOD * D), 0.5 * D ** -0.5),
        'b_mod': nrm(ks[12], (DEPTH, N_MOD * D), 0.02),
        'ffn_gu': nrm(ks[13], (DEPTH, 2, D, 2 * FFN_DIM), D ** -0.5),
        'ffn_down': nrm(ks[14], (DEPTH, 2, FFN_DIM, D), FFN_DIM ** -0.5),
        'w_in': nrm(ks[15], (DEPTH, D, IN_COLS), D ** -0.5),
        'w_out': nrm(ks[16], (DEPTH, 4 * GROUP_WIDTH, D), (4 * GROUP_WIDTH) ** -0.5),
        'na_qk_norm': 1.0 + nrm(ks[17], (DEPTH, 2, HEAD_DIM), 0.02),
        'na_rpb': nrm(ks[18], (DEPTH, NA_HEADS, 2 * NA_WIN_ROWS - 1, 2 * NA_WIN_COLS - 1), 0.1),
        'gla_gate_up': nrm(ks[19], (DEPTH, 2, GLA_GATE_RANK, GLA_HEADS * GLA_DK), GLA_GATE_RANK ** -0.5),
        'gla_gate_bias': nrm(ks[20], (DEPTH, 2, GLA_HEADS * GLA_DK), 0.1),
        'gla_out_norm': 1.0 + nrm(ks[21], (DEPTH, GLA_DV), 0.02),
        'gqa_qk_norm': 1.0 + nrm(ks[22], (DEPTH, 2, HEAD_DIM), 0.02),
        'gdn_conv': nrm(ks[23], (DEPTH, CONV_K, 3 * GDN_HEADS * HEAD_DIM), CONV_K ** -0.5),
        'gdn_a_log': jnp.log(jax.random.uniform(ks[24], (DEPTH, 2, GDN_HEADS), jnp.float32, 1.0, 16.0)),
        'gdn_dt_bias': dt + jnp.log(-jnp.expm1(-dt)),
        'gdn_out_norm': 1.0 + nrm(ks[26], (DEPTH, GDN_DV), 0.02),
    }


def reference(x_prompt, x_sample, cache_na_k, cache_na_v, cache_gqa_k, cache_gqa_v, state_gla, state_gdn,
              c, c_ctx, norm_g, w_mod, b_mod, ffn_gu, ffn_down, w_in, w_out, na_qk_norm, na_rpb,
              gla_gate_up, gla_gate_bias, gla_out_norm, gqa_qk_norm, gdn_conv, gdn_a_log, gdn_dt_bias,
              gdn_out_norm):
    P = dict(norm_g=norm_g, w_mod=w_mod, b_mod=b_mod, ffn_gu=ffn_gu, ffn_down=ffn_down, w_in=w_in,
             w_out=w_out, na_qk_norm=na_qk_norm, na_rpb=na_rpb, gla_gate_up=gla_gate_up,
             gla_gate_bias=gla_gate_bias, gla_out_norm=gla_out_norm, gqa_qk_norm=gqa_qk_norm,
             gdn_conv=gdn_conv, gdn_a_log=gdn_a_log, gdn_dt_bias=gdn_dt_bias, gdn_out_norm=gdn_out_norm)

    xp = x_prompt
    na_k_l, na_v_l, gqa_k_l, gqa_v_l, gla_l, gdn_l = [], [], [], [], [], []
    for l in range(DEPTH):
        xp, (nk, nv, gk, gv, sg, sd) = trunk_layer(xp, c_ctx[None, :], P, l, None)
        na_k_l.append(nk)
        na_v_l.append(nv)
        gqa_k_l.append(gk)
        gqa_v_l.append(gv)
        gla_l.append(sg)
        gdn_l.append(sd)
    new_na_k = jnp.stack(na_k_l, axis=1)
    new_na_v = jnp.stack(na_v_l, axis=1)
    new_gqa_k = jnp.stack(gqa_k_l, axis=1)
    new_gqa_v = jnp.stack(gqa_v_l, axis=1)
    new_state_gla = jnp.stack(gla_l, axis=1)
    new_state_gdn = jnp.stack(gdn_l, axis=1)

    xs = x_sample
    for l in range(DEPTH):
        ctx = dict(na_k=cache_na_k[:, l], na_v=cache_na_v[:, l], gqa_k=cache_gqa_k[:, l],
                   gqa_v=cache_gqa_v[:, l], gla=state_gla[:, l], gdn=state_gdn[:, l])
        xs, _ = trunk_layer(xs, c, P, l, ctx)

    return (xp, xs, new_na_k, new_na_v, new_gqa_k, new_gqa_v, new_state_gla, new_state_gdn)
```

```python
import numpy as np
import concourse.bass as bass
import concourse.mybir as mybir
from concourse.bass_utils import run_bass_kernel_spmd

F32 = mybir.dt.float32
BF16 = mybir.dt.bfloat16
AF = mybir.ActivationFunctionType
ALU = mybir.AluOpType
AX = mybir.AxisListType

SAME_ENGINE_SYNC = True
CM_NAMES = ["ident", "ones128", "ones1", "RT", "Ublk", "Lblk", "Bones", "NM_le", "NM_lt", "NM_ge", "NM_gt", "M_ge", "M_le"]
NCM = len(CM_NAMES)
CMI = {n: i for i, n in enumerate(CM_NAMES)}
NSV = 6 + 16


class T:
    __slots__ = ("name", "lw", "rd", "scr", "excl")

    def __init__(self, name, scr=False, init=None):
        self.name = name
        self.lw = {}
        self.rd = dict(init) if init else {}
        self.scr = scr
        self.excl = False


class SemKey:
    def __init__(self, name):
        self.name = name
        self.count = 0
        self.sem = None


class Op:
    __slots__ = ("eng", "fn", "deps", "signal", "count", "key", "idx")


class Sched:
    ENGS = ("pe", "act", "dve", "pool", "sp")

    def __init__(self, nc):
        self.nc = nc
        self.ops = {e: [] for e in self.ENGS}
        self.keys = []
        self.scr_marks = {}

    def key(self, name):
        k = SemKey(name)
        self.keys.append(k)
        return k

    def gkey(self, name):
        if not hasattr(self, "_gk"):
            self._gk = {}
        if name not in self._gk:
            self._gk[name] = self.key(name)
        return self._gk[name]

    def tiles(self, name, *dims, scr=False):
        if not dims:
            return T(name, scr, self.scr_marks if scr else None)
        return [self.tiles("%s_%d" % (name, i), *dims[1:], scr=scr) for i in range(dims[0])]

    def _add(self, deps, m, eng):
        if m is None:
            return
        if m[0] == "dma":
            k = m[1]
            deps[("dma", k)] = max(deps.get(("dma", k), 0), k.count)
        else:
            e, idx = m[1], m[2]
            if e == eng and (eng == "pe" or not SAME_ENGINE_SYNC):
                return
            deps[("eng", e)] = max(deps.get(("eng", e), -1), idx)

    def op(self, eng, fn, r=(), w=(), key=None, samesync=True):
        o = Op()
        o.eng, o.fn, o.signal, o.count, o.key = eng, fn, False, 0, key
        o.idx = len(self.ops[eng])
        deps = {}
        for t in r:
            for m in t.lw.values():
                self._add(deps, m, eng)
            if t.excl:
                for m in t.rd.values():
                    if m[0] == "eng" and m[1] != eng:
                        self._add(deps, m, eng)
        for t in w:
            for m in t.lw.values():
                self._add(deps, m, eng)
            for m in t.rd.values():
                self._add(deps, m, eng)
        o.deps = deps
        if key is not None:
            key.count += 16
            m = ("dma", key, key.count)
            mk = ("dma", key)
        else:
            m = ("eng", eng, o.idx)
            mk = ("eng", eng)
        for t in w:
            t.lw[mk] = m
            t.rd = {}
            if t.scr:
                self.scr_marks[mk] = m
        for t in r:
            t.rd[mk] = m
            if t.scr:
                self.scr_marks[mk] = m
        self.ops[eng].append(o)
        return o

    def emit(self):
        nc = self.nc
        from contextlib import ExitStack
        for e in self.ENGS:
            for o in self.ops[e]:
                for (kind, k), v in o.deps.items():
                    if kind == "eng":
                        self.ops[k][v].signal = True
        for e in self.ENGS:
            c = 0
            for o in self.ops[e]:
                if o.signal and o.key is None:
                    c += 1
                o.count = c
        with ExitStack() as st:
            engsem = {e: st.enter_context(nc.semaphore("sem_" + e)) for e in self.ENGS}
            for k in self.keys:
                k.sem = st.enter_context(nc.semaphore("k_" + k.name))
            block = st.enter_context(nc.Block())
            bname = {"pe": "tensor", "act": "scalar", "dve": "vector", "pool": "gpsimd", "sp": "sync"}
            for eng in self.ENGS:
                ops = self.ops[eng]
                if not ops:
                    continue

                def body(e, ops=ops):
                    known = {}
                    for o in ops:
                        for (kind, k), v in o.deps.items():
                            if kind == "eng":
                                sem, val, kid = engsem[k], self.ops[k][v].count, k
                            else:
                                sem, val, kid = k.sem, v, k.name
                            if known.get(kid, 0) < val:
                                e.wait_ge(sem, val)
                                known[kid] = val
                        ins = o.fn(e)
                        if o.key is not None:
                            ins.then_inc(o.key.sem, 16)
                        elif o.signal:
                            ins.then_inc(engsem[o.eng], 1)

                getattr(block, bname[eng])(body)


class Cfg:
    def __init__(self, depth=4, F=5504, mixers=True):
        self.depth = depth
        self.D = 2048
        self.KC = 16
        self.F = F
        self.FC = F // 128
        self.NP = 2
        self.TP = 256
        self.TS = 1024
        self.NT = self.NP * self.TP + self.TS
        self.NTT = self.NT // 512
        self.IN_COLS = 6192
        self.mixers = mixers
        self.G = 4
        self.mix = "ABCD"


def tt_cond(tt):
    return 0 if tt == 0 else 1


def build(cfg):
    nc = bass.Bass("TRN2", target_bir_lowering=False)
    D, KC, NT, NTT, L, FC = cfg.D, cfg.KC, cfg.NT, cfg.NTT, cfg.depth, cfg.FC

    def din(name, shape, dt=F32):
        return nc.dram_tensor(name, list(shape), dt, kind="ExternalInput").ap()

    def dout(name, shape, dt=F32):
        return nc.dram_tensor(name, list(shape), dt, kind="ExternalOutput").ap()

    xin = din("xin", [NT, D])
    condT = din("condT", [128, KC * 2])
    normgT = din("normgT", [128, L * 3 * KC])
    bmodT = din("bmodT", [128, L * 144 * 2])
    ident_d = din("ident", [128, 128])
    w_mod = din("w_mod", [L, D, 9 * D])
    ffn_gu = din("ffn_gu", [L, 2, D, 2 * cfg.F])
    ffn_down = din("ffn_down", [L, 2, cfg.F, D])
    y = dout("y", [NT, D])
    MX = cfg.mixers
    if MX:
        w_in = din("w_in", [L, D, 6192])
        w_out = din("w_out", [L, D, D])
        cmat_d = din("cmat", [128, NCM * 128])
        rope_d = din("rope", [128, 2 * 1024])
        svec_d = din("svec", [128, L * NSV])
        convw_d = din("convw", [128, L * 12 * 5])
        gup_d = din("gup", [64, L * 512])
        rpbx_d = din("rpbx", [128, L * 4 * 14 * 64])
        c_nak = din("c_nak", [L, 4, 256, 128])
        c_nav = din("c_nav", [L, 4, 256, 128])
        c_gqk = din("c_gqk", [L, 2, 256, 128])
        c_gqv = din("c_gqv", [L, 2, 256, 128])
        s_gla = din("s_gla", [L, 2, 4, 64, 128])
        s_gdn = din("s_gdn", [L, 2, 4, 128, 128])
        o_nak = dout("o_nak", [2, L, 4, 256, 128])
        o_nav = dout("o_nav", [2, L, 4, 256, 128])
        o_gqk = dout("o_gqk", [2, L, 2, 256, 128])
        o_gqv = dout("o_gqv", [2, L, 2, 256, 128])
        o_gla = dout("o_gla", [2, L, 2, 4, 64, 128])
        o_gdn = dout("o_gdn", [2, L, 2, 4, 128, 128])

    S = Sched(nc)
    from contextlib import ExitStack
    st = ExitStack()

    ARENA_BYTES = 212736
    arena = st.enter_context(nc.sbuf_tensor("arena", [128, ARENA_BYTES // 4], F32))
    apos = [0]

    def alloc(nbytes):
        off = apos[0]
        apos[0] += (nbytes + 31) // 32 * 32
        assert apos[0] <= ARENA_BYTES, "arena overflow %d" % apos[0]
        return off

    def view(off, nbytes, dt, pattern=None, **kw):
        a = arena[:, off // 4:(off + nbytes) // 4]
        if dt is not F32:
            a = a.bitcast(dt)
        if pattern:
            a = a.rearrange(pattern, **kw)
        return a

    def new(shape, dt):
        esz = 4 if dt is F32 else 2
        n = int(np.prod(shape)) * esz
        off = alloc(n)
        if len(shape) == 1:
            return view(off, n, dt)
        if len(shape) == 2:
            return view(off, n, dt, "p (a b) -> p a b", a=shape[0])
        return view(off, n, dt, "p (a b c) -> p a b c", a=shape[0], b=shape[1])

    xT = new([KC, NT], F32)
    hT = new([KC, NT], BF16)
    ident = new([128], F32) if not cfg.mixers else None
    normg = new([3, KC], F32)
    cT = new([KC, 2], F32)
    scT = new([KC, 2], BF16)
    modT = new([144, 2], F32)
    AB = new([9, KC, 2], F32)
    if MX:
        cmat = new([NCM, 128], F32)
        svec = new([NSV], F32)
        convw = new([12, 5], F32)
        cvec = new([4], F32)
        ident = cmat[:, CMI["ident"], :]
        ones_bf = new([128], BF16)
        cmat_bf = new([NCM, 128], BF16)
    scratch0 = apos[0]
    SCR = ARENA_BYTES - scratch0

    def scratch_reset():
        apos[0] = scratch0

    banks = [st.enter_context(nc.psum_tensor("ps%d" % i, [128, 512], F32)) for i in range(8)]
    bankT = S.tiles("bank", 8)
    for _bt in bankT:
        _bt.excl = True
    bank_rr = [0]

    def next_bank():
        b = bank_rr[0]
        bank_rr[0] = (b + 1) % 8
        return b

    xT_t = S.tiles("xT", KC, NTT)
    hT_t = S.tiles("hT", KC, NTT)
    const_t = S.tiles("const")
    mod_t = S.tiles("mod")
    ab_t = S.tiles("ab")

    kc_const = S.key("const")
    ident_t, normg_t, cT_t, scT_t, ones_t, eps_t = [S.tiles(n) for n in
                                                   ("ident", "normg", "cT", "scT", "ones", "eps")]
    if not cfg.mixers:
        S.op("sp", lambda e: e.dma_start(out=ident, in_=ident_d), w=[ident_t], key=S.gkey("const1"))
    S.op("sp", lambda e: e.dma_start(out=cT.rearrange("p a b -> p (a b)"), in_=condT), w=[cT_t], key=S.gkey("const3"))
    S.op("act", lambda e: e.activation(scT, cT, AF.Silu), r=[cT_t], w=[scT_t])

    def load_x():
        scratch_reset()
        xb = [new([D], F32) for _ in range(2)]
        xb_t = S.tiles("xb", 2, scr=True)
        xk = [S.key("xb0"), S.key("xb1")]
        ev = 0
        for ti in range(NT // 128):
            b = ti % 2
            S.op("sp", lambda e, b=b, ti=ti: e.dma_start(out=xb[b], in_=xin[ti * 128:(ti + 1) * 128, :]),
                 w=[xb_t[b]], key=xk[b])
            tt = ti // 4
            for q in range(4):
                bk = next_bank()
                for j in range(4):
                    dc = q * 4 + j
                    S.op("pe", lambda e, bk=bk, j=j, b=b, dc=dc: e.transpose(
                        banks[bk][:, j * 128:(j + 1) * 128], xb[b][:, dc * 128:(dc + 1) * 128], ident),
                        r=[xb_t[b], ident_t], w=[bankT[bk]])
                eng = "dve" if ev % 2 == 0 else "act"
                ev += 1
                dst = xT[:, q * 4:(q + 1) * 4, ti * 128:(ti + 1) * 128]
                src = banks[bk][:, :].rearrange("p (a b) -> p a b", a=4)
                if eng == "dve":
                    S.op("dve", lambda e, dst=dst, src=src: e.tensor_copy(dst, src),
                         r=[bankT[bk]], w=[xT_t[q * 4 + j][tt] for j in range(4)])
                else:
                    S.op("act", lambda e, dst=dst, src=src: e.activation(dst, src, AF.Copy),
                         r=[bankT[bk]], w=[xT_t[q * 4 + j][tt] for j in range(4)])

    def adaln(l):
        scratch_reset()
        NB = 256
        S.op("sp", lambda e: e.dma_start(out=normg.rearrange("p a b -> p (a b)"), in_=normgT[:, l * 3 * KC:(l + 1) * 3 * KC]),
             w=[normg_t], key=S.gkey("const2"))
        if MX:
            S.op("sp", lambda e: e.dma_start(out=svec, in_=svec_d[:, l * NSV:(l + 1) * NSV]), w=[svec_t], key=S.gkey("const5"))
            S.op("sp", lambda e: e.dma_start(out=convw.rearrange("p a b -> p (a b)"), in_=convw_d[:, l * 60:(l + 1) * 60]),
                 w=[convw_t], key=S.gkey("const6"))
        bmod = new([144, 2], F32)
        bmod_t = S.tiles("bmod", scr=True)
        S.op("sp", lambda e: e.dma_start(out=bmod.rearrange("p a b -> p (a b)"), in_=bmodT[:, l * 288:(l + 1) * 288]),
             w=[bmod_t], key=S.gkey("bmod"))
        wb = [new([KC, NB], BF16) for _ in range(2)]
        wb_t = S.tiles("wmb", 2, scr=True)
        wk = [S.gkey("wmb0"), S.gkey("wmb1")]
        bk = next_bank()
        nblk = 9 * D // NB
        for blk in range(nblk):
            b = blk % 2
            src = w_mod[l, :, blk * NB:(blk + 1) * NB].rearrange("(kc p) n -> p kc n", p=128)
            S.op("pool", lambda e, b=b, src=src: e.dma_start(out=wb[b], in_=src), w=[wb_t[b]], key=wk[b])
            for s in range(NB // 128):
                cb = blk * (NB // 128) + s
                for kc in range(KC):
                    S.op("pe", lambda e, b=b, s=s, kc=kc, cb=cb: e.matmul(
                        banks[bk][:, cb * 2:cb * 2 + 2], wb[b][:, kc, s * 128:(s + 1) * 128], scT[:, kc, :],
                        start=(kc == 0), stop=(kc == KC - 1)),
                        r=[wb_t[b], scT_t], w=[bankT[bk]])
        S.op("dve", lambda e: e.tensor_tensor(
            modT, banks[bk][:, 0:288].rearrange("p (a b) -> p a b", b=2), bmod, ALU.add),
            r=[bankT[bk], bmod_t], w=[mod_t])
        for i in range(3):
            sh = modT[:, (3 * i) * KC:(3 * i + 1) * KC, :]
            sc = modT[:, (3 * i + 1) * KC:(3 * i + 2) * KC, :]
            g = modT[:, (3 * i + 2) * KC:(3 * i + 3) * KC, :]
            ng = normg[:, i, :]
            S.op("dve", lambda e, i=i, sc=sc, ng=ng: e.scalar_tensor_tensor(
                AB[:, 3 * i, :, :], sc, 1.0, ng.unsqueeze(2).to_broadcast([128, KC, 2]), ALU.add, ALU.mult),
                r=[mod_t, normg_t], w=[ab_t])
            S.op("dve", lambda e, i=i, sh=sh: e.tensor_copy(AB[:, 3 * i + 1, :, :], sh), r=[mod_t], w=[ab_t])
            gs = 1.0 if i == 1 else 0.5
            S.op("dve", lambda e, i=i, g=g, gs=gs: e.tensor_scalar(
                AB[:, 3 * i + 2, :, :], g, gs, None, ALU.mult), r=[mod_t], w=[ab_t])

    def norm_mod(i):
        scratch_reset()
        sq = [new([512], F32) for _ in range(2)]
        sq_t = S.tiles("sq", 2, scr=True)
        rstd = [new([512], F32) for _ in range(2)]
        rstd_t = S.tiles("rstd", 2, scr=True)
        tmp = [new([512], F32) for _ in range(2)]
        tmp_t = S.tiles("ntmp", 2, scr=True)
        n = 0
        for tt in range(NTT):
            cs = tt_cond(tt)
            bk = next_bank()
            for dc in range(KC):
                b = n % 2
                n += 1
                S.op("act", lambda e, b=b, dc=dc, tt=tt: e.activation(
                    sq[b], xT[:, dc, tt * 512:(tt + 1) * 512], AF.Square), r=[xT_t[dc][tt]], w=[sq_t[b]])
                S.op("pe", lambda e, b=b, dc=dc, bk=bk: e.matmul(
                    banks[bk][:, :], ones_mat, sq[b],
                    start=(dc == 0), stop=(dc == KC - 1)), r=[sq_t[b], ones_t], w=[bankT[bk]])
            rb = tt % 2
            S.op("act", lambda e, rb=rb, bk=bk: e.activation(
                rstd[rb], banks[bk][:, :], AF.Sqrt, bias=eps_c, scale=1.0), r=[bankT[bk], eps_t], w=[rstd_t[rb]])
            S.op("dve", lambda e, rb=rb: e.reciprocal(rstd[rb], rstd[rb]), r=[rstd_t[rb]], w=[rstd_t[rb]])
            for dc in range(KC):
                b = n % 2
                n += 1
                S.op("dve", lambda e, b=b, dc=dc, tt=tt, rb=rb, cs=cs: e.scalar_tensor_tensor(
                    tmp[b], xT[:, dc, tt * 512:(tt + 1) * 512], AB[:, 3 * i, dc, cs:cs + 1], rstd[rb],
                    ALU.mult, ALU.mult), r=[xT_t[dc][tt], rstd_t[rb], ab_t], w=[tmp_t[b]])
                S.op("act", lambda e, b=b, dc=dc, tt=tt, cs=cs: e.activation(
                    hT[:, dc, tt * 512:(tt + 1) * 512], tmp[b], AF.Identity,
                    bias=AB[:, 3 * i + 1, dc, cs:cs + 1], scale=1.0), r=[tmp_t[b], ab_t], w=[hT_t[dc][tt]])

    ones_mat = new([128], F32)
    eps_c = new([1], F32)
    S.op("dve", lambda e: e.memset(eps_c, 1e-6), w=[eps_t])
    S.op("dve", lambda e: e.memset(ones_mat, 1.0 / D), w=[ones_t])
    scratch0 = apos[0]
    SCR = ARENA_BYTES - scratch0

    def ffn(l, f, gi):
        scratch_reset()
        G = cfg.G
        act = new([G, NT], BF16)
        act_t = S.tiles("act", G, NTT, scr=True)
        wd = new([G, D], BF16)
        wd_t = S.tiles("wd", G, scr=True)
        wdk = [S.gkey("wd%d" % c) for c in range(G)]
        wg = [new([KC, 256], BF16) for _ in range(2)]
        wg_t = S.tiles("wg", 2, scr=True)
        wgk = [S.gkey("wg%d" % b) for b in range(2)]
        sg = [new([512], F32) for _ in range(2)]
        sg_t = S.tiles("sg", 2, scr=True)
        F = cfg.F
        nsg = 0
        ups = 0
        for g0 in range(0, FC, G):
            chunks = list(range(g0, min(g0 + G, FC)))
            for ci, c in enumerate(chunks):
                b = c % 2
                srcg = ffn_gu[l, f, :, c * 128:(c + 1) * 128].rearrange("(kc p) n -> p kc n", p=128)
                srcu = ffn_gu[l, f, :, F + c * 128:F + (c + 1) * 128].rearrange("(kc p) n -> p kc n", p=128)
                S.op("pool", lambda e, b=b, srcg=srcg: e.dma_start(out=wg[b][:, :, 0:128], in_=srcg),
                     w=[wg_t[b]], key=wgk[b])
                S.op("pool", lambda e, b=b, srcu=srcu: e.dma_start(out=wg[b][:, :, 128:256], in_=srcu),
                     w=[wg_t[b]], key=wgk[b])
                S.op("pool", lambda e, ci=ci, c=c: e.dma_start(out=wd[:, ci, :], in_=ffn_down[l, f, c * 128:(c + 1) * 128, :]),
                     w=[wd_t[ci]], key=wdk[ci])
                for tt in range(NTT):
                    bg = (ups % 2) * 2
                    bu = bg + 1
                    ups += 1
                    for kc in range(KC):
                        S.op("pe", lambda e, b=b, kc=kc, tt=tt, bg=bg: e.matmul(
                            banks[bg][:, :], wg[b][:, kc, 0:128], hT[:, kc, tt * 512:(tt + 1) * 512],
                            start=(kc == 0), stop=(kc == KC - 1)), r=[wg_t[b], hT_t[kc][tt]], w=[bankT[bg]])
                    for kc in range(KC):
                        S.op("pe", lambda e, b=b, kc=kc, tt=tt, bu=bu: e.matmul(
                            banks[bu][:, :], wg[b][:, kc, 128:256], hT[:, kc, tt * 512:(tt + 1) * 512],
                            start=(kc == 0), stop=(kc == KC - 1)), r=[wg_t[b], hT_t[kc][tt]], w=[bankT[bu]])
                    sb = nsg % 2
                    nsg += 1
                    S.op("act", lambda e, sb=sb, bg=bg: e.activation(sg[sb], banks[bg][:, :], AF.Silu),
                         r=[bankT[bg]], w=[sg_t[sb]])
                    S.op("dve", lambda e, sb=sb, bu=bu, ci=ci, tt=tt: e.tensor_tensor(
                        act[:, ci, tt * 512:(tt + 1) * 512], sg[sb], banks[bu][:, :], ALU.mult),
                        r=[sg_t[sb], bankT[bu]], w=[act_t[ci][tt]])
            for j in range(KC):
                for tt in range(NTT):
                    cs = tt_cond(tt)
                    bk = 4 + (next_bank() % 4)
                    for ci, c in enumerate(chunks):
                        S.op("pe", lambda e, bk=bk, ci=ci, j=j, tt=tt: e.matmul(
                            banks[bk][:, :], wd[:, ci, j * 128:(j + 1) * 128], act[:, ci, tt * 512:(tt + 1) * 512],
                            start=(ci == 0), stop=(ci == len(chunks) - 1)),
                            r=[wd_t[ci], act_t[ci][tt]], w=[bankT[bk]])
                    S.op("dve", lambda e, bk=bk, j=j, tt=tt, cs=cs: e.scalar_tensor_tensor(
                        xT[:, j, tt * 512:(tt + 1) * 512], banks[bk][:, :], AB[:, gi, j, cs:cs + 1],
                        xT[:, j, tt * 512:(tt + 1) * 512], ALU.mult, ALU.add),
                        r=[bankT[bk], ab_t, xT_t[j][tt]], w=[xT_t[j][tt]])

    out_keys = []
    dbg_outs = []

    def dump(name, ap, tiles):
        if not getattr(cfg, "dbg", False):
            return
        shp = list(ap.shape)
        flat = ap
        o = nc.dram_tensor("dbg_" + name, shp, ap.dtype, kind="ExternalOutput").ap()
        k = S.key("dbg_" + name)
        out_keys.append(k)
        dbg_outs.append("dbg_" + name)
        S.op("sp", lambda e: e.dma_start(out=o, in_=ap), r=tiles, key=k)

    def alltiles(tt):
        return [t for row in tt for t in row]

    gkey = S.gkey
    if MX:
        svec_t, convw_t, cvec_t, onesbf_t = [S.tiles(n) for n in ("svec", "convw", "cvec", "onesbf")]
        cm_t = ident_t
        S.op("sp", lambda e: e.dma_start(out=cmat.rearrange("p a b -> p (a b)"), in_=cmat_d), w=[cm_t], key=S.gkey("const4"))
        S.op("dve", lambda e: e.memset(cvec[:, 0:1], 1e-6), w=[cvec_t])
        S.op("dve", lambda e: e.memset(cvec[:, 1:2], 1.0), w=[cvec_t])
        S.op("dve", lambda e: e.memset(cvec[:, 2:4], 0.0), w=[cvec_t])
        S.op("dve", lambda e: e.memset(cvec[0:64, 2:3], 1.0), w=[cvec_t])
        S.op("dve", lambda e: e.memset(cvec[64:128, 3:4], 1.0), w=[cvec_t])
        S.op("dve", lambda e: e.memset(ones_bf, 1.0), w=[onesbf_t])
        cmb_t = S.tiles("cmb")
        S.op("act", lambda e: e.activation(cmat_bf, cmat, AF.Copy), r=[cm_t], w=[cmb_t])

        def CMB(name):
            return cmat_bf[:, CMI[name], :]
        EPS = cvec[:, 0:1]
        ONE = cvec[:, 1:2]

        def CM(name):
            return cmat[:, CMI[name], :]

    def mm(out, lhsT, rhs, r, w, start=True, stop=True):
        S.op("pe", lambda e: e.matmul(out, lhsT, rhs, start=start, stop=stop), r=r, w=w)

    def tr(out, in_, r, w):
        S.op("pe", lambda e: e.transpose(out, in_, cmat[0:in_.shape[0], CMI["ident"], 0:in_.shape[0]]), r=list(r) + [cm_t], w=w)

    def act(out, in_, func, r, w, bias=None, scale=1.0):
        if bias is None:
            S.op("act", lambda e: e.activation(out, in_, func, scale=scale), r=r, w=w)
        else:
            S.op("act", lambda e: e.activation(out, in_, func, bias=bias, scale=scale), r=r, w=w)

    def vtt(out, a, b, op, r, w):
        S.op("dve", lambda e: e.tensor_tensor(out, a, b, op), r=r, w=w)

    def vstt(out, in0, scalar, in1, op0, op1, r, w):
        S.op("dve", lambda e: e.scalar_tensor_tensor(out, in0, scalar, in1, op0, op1), r=r, w=w)

    def vts(out, in0, s1, s2, op0, op1, r, w):
        if s2 is None:
            S.op("dve", lambda e: e.tensor_scalar(out, in0, s1, None, op0), r=r, w=w)
        else:
            S.op("dve", lambda e: e.tensor_scalar(out, in0, s1, s2, op0, op1), r=r, w=w)

    def vcp(out, in_, r, w):
        S.op("dve", lambda e: e.tensor_copy(out, in_), r=r, w=w)

    def vrecip(out, in_, r, w):
        S.op("dve", lambda e: e.reciprocal(out, in_), r=r, w=w)

    cp_rr = [0]

    def anycp(out, in_, r, w):
        cp_rr[0] += 1
        if cp_rr[0] % 2:
            vcp(out, in_, r, w)
        else:
            act(out, in_, AF.Copy, r, w)

    def BK(exclude=()):
        while True:
            b = next_bank()
            if bankT[b] not in exclude:
                return banks[b], bankT[b]

    def stile(name, *dims):
        return S.tiles(name, *dims, scr=True)

    TB = [(0, 512, 0), (512, 512, 1), (1024, 512, 1)]
    SEQS = [(0, 256), (256, 256), (512, 1024)]

    def mixers(l):
        scratch_reset()
        uid = [0]

        def key(nm):
            return gkey(nm)

        def sv(col):
            return svec[:, col:col + 1]

        def wpiece(buf, buf_t, cols, src=None):
            k = gkey("wp_" + buf_t.name)
            off = 0
            for (c0, n) in cols:
                sl = w_in[l, :, c0:c0 + n].rearrange("(kc p) n -> p kc n", p=128)
                S.op("pool", lambda e, sl=sl, off=off, n=n: e.dma_start(out=buf[:, :, off:off + n], in_=sl),
                     w=[buf_t], key=k)
                off += n

        def proj_feat(buf, buf_t, c0, M, tb, out_bank):
            t0, n, _ = TB[tb]
            bk, bt = out_bank
            for kc in range(KC):
                mm(bk[0:M, 0:n], buf[:, kc, c0:c0 + M], hT[:, kc, t0:t0 + n], [buf_t, hT_t[kc][tb]], [bt],
                   start=(kc == 0), stop=(kc == KC - 1))

        def proj_tok(buf, buf_t, c0, N, tok0, out_ap, bt):
            tbs = sorted({tok0 // 512, (tok0 + 127) // 512})
            for kc in range(KC):
                mm(out_ap, hT[:, kc, tok0:tok0 + 128], buf[:, kc, c0:c0 + N], [buf_t] + [hT_t[kc][tb] for tb in tbs], [bt],
                   start=(kc == 0), stop=(kc == KC - 1))

        ftmp = [new([512], F32)]
        ftmp_t = stile("ftmp", 1)
        frs = [new([512], F32)]
        frs_t = stile("frs", 1)

        def fnorm(src, src_t, n, out, out_t, gain, mean=True, P=128):
            i = 0
            fb = ftmp[i].bitcast(BF16)
            act(fb[0:P, 0:n], src, AF.Square, [src_t], [ftmp_t[i]])
            bk, bt = BK()
            mm(bk[0:P, 0:n], ones_bf[0:P, 0:P], fb[0:P, 0:n], [ftmp_t[i], onesbf_t], [bt])
            act(frs[i][0:P, 0:n], bk[0:P, 0:n], AF.Sqrt, [bt, cvec_t], [frs_t[i]], bias=EPS[0:P, :],
                scale=(1.0 / 128 if mean else 1.0))
            vrecip(frs[i][0:P, 0:n], frs[i][0:P, 0:n], [frs_t[i]], [frs_t[i]])
            vstt(out, src, gain, frs[i][0:P, 0:n], ALU.mult, ALU.mult, [src_t, frs_t[i], svec_t], [out_t])

        wo = new([D], BF16)
        wo_t = stile("wo")

        def out_proj(row0, mg, mg_t):
            k = gkey("wo")
            S.op("pool", lambda e: e.dma_start(out=wo, in_=w_out[l, row0:row0 + 128, :]), w=[wo_t], key=k)
            for j in range(KC):
                for tb, (t0, n, cs) in enumerate(TB):
                    bk, bt = BK()
                    mm(bk[:, 0:n], wo[:, j * 128:(j + 1) * 128], mg[:, t0:t0 + n], [wo_t, mg_t], [bt])
                    vstt(xT[:, j, t0:t0 + n], bk[:, 0:n], AB[:, 5, j, cs:cs + 1], xT[:, j, t0:t0 + n],
                         ALU.mult, ALU.add, [bt, ab_t, xT_t[j][tb]], [xT_t[j][tb]])

        pt_rr = [0]
        ATT_SCALE = 128.0 ** -0.5

        def attention(qT_ap, q_t, n, ktiles, out_ap, out_t, pt, pt_t, rden, rden_t):
            bo, bot = BK()
            bd, bdt = BK()
            nk = len(ktiles)
            for i, (kT_ap, v_ap, tl, eb) in enumerate(ktiles):
                bs, bst = BK(exclude=(bot, bdt))
                mm(bs[:, 0:n], kT_ap, qT_ap, list(tl) + [q_t], [bst])
                pi = pt_rr[0] % 2
                pt_rr[0] += 1
                act(pt[pi][:, 0:n], bs[:, 0:n], AF.Exp, [bst], [pt_t[pi]], scale=ATT_SCALE)
                if eb is not None:
                    eb_ap, eb_t = eb
                    vtt(pt[pi][:, 0:n], pt[pi][:, 0:n], eb_ap, ALU.mult, [pt_t[pi], eb_t], [pt_t[pi]])
                mm(bo[:, 0:n], v_ap, pt[pi][:, 0:n], list(tl) + [pt_t[pi]], [bot], start=(i == 0), stop=(i == nk - 1))
                mm(bd[:, 0:n], ones_bf, pt[pi][:, 0:n], [onesbf_t, pt_t[pi]], [bdt], start=(i == 0), stop=(i == nk - 1))
            vrecip(rden[:, 0:n], bd[:, 0:n], [bdt], [rden_t])
            vtt(out_ap, bo[:, 0:n], rden[:, 0:n], ALU.mult, [bot, rden_t], [out_t])

        stg = [new([128], F32) for _ in range(2)]
        stg_t = stile("stg", 2)
        stg_k = [gkey("stg0"), gkey("stg1")]
        stg_rr = [0]

        def store_tok(psum_ap, bt, dst):
            i = stg_rr[0] % 2
            stg_rr[0] += 1
            anycp(stg[i], psum_ap, [bt], [stg_t[i]])
            S.op("sp", lambda e: e.dma_start(out=dst, in_=stg[i]), r=[stg_t[i]], key=stg_k[i])

        mg = [new([NT], BF16)]
        mg_t = stile("mg", 1)
        mg_rr = [0]
        base_mark = apos[0]

        def mgsel():
            return 0

        def attn_mixer(kind):
            apos[0] = base_mark
            if kind == "A":
                qc0, kc0, vc0, nq, nkv, gq, gk, row0 = 0, 512, 1024, 4, 4, sv(0), sv(1), 0
                ck_d, cv_d, ok_d, ov_d = c_nak, c_nav, o_nak, o_nav
            else:
                qc0, kc0, vc0, nq, nkv, gq, gk, row0 = 3104, 3616, 3872, 4, 2, sv(2), sv(3), 1024
                ck_d, cv_d, ok_d, ov_d = c_gqk, c_gqv, o_gqk, o_gqv
            qpk = nq // nkv
            nf = [new([512], F32)]
            nf_t = stile("nf", 1)
            wpA = new([KC, 256], BF16)
            wpA_t = stile("wpA" + kind)
            wpB = new([KC, 128], BF16)
            wpB_t = stile("wpB" + kind)
            pt = [new([512], BF16) for _ in range(2)]
            pt_t = stile("pt", 2)
            rden = new([512], F32)
            rden_t = stile("rden")
            kT = new([NT], BF16)
            kT_t = stile("kT")
            nvt = 19 if kind == "A" else 12
            vtok = new([nvt, 128], BF16)
            vtok_t = stile("vtok")
            ckT = new([256], BF16)
            ckT_t = stile("ckT")
            cv = new([2, 128], BF16)
            cv_t = stile("cv")
            ckl = new([2, 128], F32)
            ckl_t = stile("ckl")
            qT = new([NT], BF16)
            qT_t = stile("qT")
            nf_rr = [0]
            if kind == "C":
                rope = new([2, 512], F32)
                rope_t = stile("rope")
                kr = gkey("rope")
                rt1 = new([512], F32)
                rt1_t = stile("rt1")
            else:
                eb = new([14, 64], F32)
                eb_t = stile("eb")

            def normed(buf, buf_t, c0, tb, gain, dst_bf, dst_t, do_rope, want_f32):
                t0, n, _ = TB[tb]
                bk = BK()
                proj_feat(buf, buf_t, c0, 128, tb, bk)
                i = 0
                fnorm(bk[0][:, 0:n], bk[1], n, nf[i][:, 0:n], nf_t[i], gain)
                if do_rope and not getattr(cfg, "norope", False):
                    s0 = t0 - 512
                    for cs_ in range(2):
                        S.op("sp", lambda e, cs_=cs_, s0=s0: e.dma_start(
                            out=rope[:, cs_, :], in_=rope_d[:, cs_ * 1024 + s0:cs_ * 1024 + s0 + 512]), w=[rope_t], key=kr)
                    kx = gkey("ropex")
                    for (a, b) in ((0, 32), (32, 0), (64, 96), (96, 64)):
                        S.op("sp", lambda e, a=a, b=b, n=n: e.dma_start(out=rt1[a:a + 32, 0:n], in_=nf[0][b:b + 32, 0:n]),
                             r=[nf_t[0]], w=[rt1_t], key=kx)
                    vtt(rt1[:, 0:n], rt1[:, 0:n], rope[:, 1, 0:n], ALU.mult, [rt1_t, rope_t], [rt1_t])
                    vtt(nf[i][:, 0:n], nf[i][:, 0:n], rope[:, 0, 0:n], ALU.mult, [nf_t[i], rope_t], [nf_t[i]])
                    vtt(dst_bf[:, t0:t0 + n], nf[i][:, 0:n], rt1[:, 0:n], ALU.add, [nf_t[i], rt1_t], [dst_t])
                else:
                    anycp(dst_bf[:, t0:t0 + n], nf[i][:, 0:n], [nf_t[i]], [dst_t])
                return nf[i], nf_t[i]

            for g in range(nkv):
                wpiece(wpA, wpA_t, [(kc0 + g * 128, 128), (vc0 + g * 128, 128)])
                for tb in range(3):
                    f, f_t = normed(wpA, wpA_t, 0, tb, gk, kT, kT_t, kind == "C" and tb > 0, tb == 0)
                    if tb == 0:
                        for s4 in range(4):
                            bk, bt = BK()
                            tr(bk[:, 0:128], f[:, s4 * 128:(s4 + 1) * 128], [f_t], [bt])
                            store_tok(bk[:, 0:128], bt, ok_d[s4 // 2, l, g, (s4 % 2) * 128:(s4 % 2) * 128 + 128, :])
                vt_list = [ti * 128 for ti in range(12)]
                if kind == "A":
                    vt_list += [512 + 64 + 128 * i for i in range(7)]
                for vi, tok0 in enumerate(vt_list):
                    bk, bt = BK()
                    proj_tok(wpA, wpA_t, 128, 128, tok0, bk[:, 0:128], bt)
                    anycp(vtok[:, vi, :], bk[:, 0:128], [bt], [vtok_t])
                    if vi < 4:
                        store_tok(bk[:, 0:128], bt, ov_d[vi // 2, l, g, (vi % 2) * 128:(vi % 2) * 128 + 128, :])
                kk = gkey("ck")
                S.op("sp", lambda e, g=g: e.dma_start(out=ckl, in_=ck_d[l, g].rearrange("(a p) d -> p a d", p=128)),
                     w=[ckl_t], key=kk)
                S.op("pool", lambda e, g=g: e.dma_start(out=cv, in_=cv_d[l, g].rearrange("(a p) d -> p a d", p=128)),
                     w=[cv_t], key=gkey("cvk"))
                for a in range(2):
                    bk, bt = BK()
                    tr(bk[:, 0:128], ckl[:, a, :], [ckl_t], [bt])
                    anycp(ckT[:, a * 128:(a + 1) * 128], bk[:, 0:128], [bt], [ckT_t])
                for hq in range(qpk):
                    h = g * qpk + hq
                    wpiece(wpB, wpB_t, [(qc0 + h * 128, 128)])
                    for tb in range(3):
                        normed(wpB, wpB_t, 0, tb, gq, qT, qT_t, kind == "C" and tb > 0, False)
                    mi = 0
                    for s_ in range(2):
                        kt = [(kT[:, s_ * 256 + a * 128:s_ * 256 + (a + 1) * 128], vtok[:, s_ * 2 + a, :], [kT_t, vtok_t], None)
                              for a in range(2)]
                        attention(qT[:, s_ * 256:(s_ + 1) * 256], qT_t, 256, kt, mg[mi][:, s_ * 256:(s_ + 1) * 256], mg_t[mi],
                                  pt, pt_t, rden, rden_t)
                    ctx_kt = [(ckT[:, a * 128:(a + 1) * 128], cv[:, a, :], [ckT_t, cv_t], None) for a in range(2)]
                    if getattr(cfg, "nolat", False):
                        pass
                    elif kind == "C":
                        loc = [(kT[:, 512 + a * 128:512 + (a + 1) * 128], vtok[:, 4 + a, :], [kT_t, vtok_t], None) for a in range(8)]
                        for qb in range(2):
                            attention(qT[:, 512 + qb * 512:512 + (qb + 1) * 512], qT_t, 512, ctx_kt + loc,
                                      mg[mi][:, 512 + qb * 512:512 + (qb + 1) * 512], mg_t[mi], pt, pt_t, rden, rden_t)
                    else:
                        ke = gkey("eb")
                        src = rpbx_d[:, ((l * 4 + h) * 14) * 64:((l * 4 + h) * 14 + 14) * 64]
                        S.op("sp", lambda e, src=src: e.dma_start(out=eb.rearrange("p a b -> p (a b)"), in_=src), w=[eb_t], key=ke)
                        act(eb, eb, AF.Exp, [eb_t], [eb_t])
                        for r_ in range(16):
                            k0 = min(max(r_ - 4, 0), 8)
                            loc = []
                            for j in range(4):
                                start = (k0 + 2 * j) * 64
                                vi = 4 + start // 128 if start % 128 == 0 else 12 + (start - 64) // 128
                                dr0 = k0 + 2 * j - r_
                                loc.append((kT[:, 512 + start:512 + start + 128], vtok[:, vi, :], [kT_t, vtok_t],
                                            (eb[:, dr0 + 7, :], eb_t)))
                            attention(qT[:, 512 + r_ * 64:512 + (r_ + 1) * 64], qT_t, 64, ctx_kt + loc,
                                      mg[mi][:, 512 + r_ * 64:512 + (r_ + 1) * 64], mg_t[mi], pt, pt_t, rden, rden_t)
                    out_proj(row0 + h * 128, mg[mi], mg_t[mi])

        if "A" in cfg.mix:
            attn_mixer("A")
        dump("mgA_%d" % l, mg[0], [mg_t[0]])
        if "C" in cfg.mix:
            attn_mixer("C")
        dump("mgC_%d" % l, mg[0], [mg_t[0]])


        def run_chains(d, h, dk, Sf, Sf_t, Sb, Sb_t, state_in, state_out, ks, pre, step_fn):
            jit = [0]
            if getattr(cfg, "bstage", 2) < 0:
                return
            for si, (st0, sn) in enumerate(SEQS):
                ci = si % 2
                if si < 2:
                    S.op("dve", lambda e, ci=ci: e.memset(Sf[0:dk, ci, :], 0.0), w=[Sf_t[ci]])
                else:
                    S.op("sp", lambda e, ci=ci: e.dma_start(out=Sf[0:dk, ci, :], in_=state_in[l, d, h]), w=[Sf_t[ci]], key=ks)
                vcp(Sb[0:dk, ci, :], Sf[0:dk, ci, :], [Sf_t[ci]], [Sb_t[ci]])
                tiles = list(range(st0 // 128, (st0 + sn) // 128))
                if d == 1:
                    tiles = tiles[::-1]
                for ti in tiles:
                    i = jit[0] % 2
                    jit[0] += 1
                    if getattr(cfg, "bstage", 2) >= 1:
                        pre(ti, i)
                    if getattr(cfg, "bstage", 2) >= 2:
                        for cc in ((0, 1) if d == 0 else (1, 0)):
                            step_fn(ti, cc, i, ci)
                if si < 2:
                    S.op("sp", lambda e, ci=ci, si=si: e.dma_start(out=state_out[si, l, d, h], in_=Sf[0:dk, ci, :]),
                         r=[Sf_t[ci]], key=ks)

        def gla_mixer():
            apos[0] = base_mark
            wpA = new([KC, 128], BF16)
            wpA_t = stile("wpAB")
            wpB = new([KC, 256], BF16)
            wpB_t = stile("wpBB")
            gfa = new([NT], BF16)
            gfa_t = stile("gfa")
            gup = new([512], BF16)
            gup_t = stile("gup")
            kg = gkey("gup")
            S.op("pool", lambda e: e.dma_start(out=gup[0:64, :], in_=gup_d[:, l * 512:(l + 1) * 512]), w=[gup_t], key=kg)
            wpiece(wpA, wpA_t, [(3072, 32)])
            S.op("dve", lambda e: e.memset(gfa[0:64, :], 1.0), w=[gfa_t])
            for tb in range(3):
                t0, n, _ = TB[tb]
                bk = BK()
                proj_feat(wpA, wpA_t, 0, 32, tb, bk)
                vcp(gfa[0:32, t0:t0 + n], bk[0][0:32, 0:n], [bk[1]], [gfa_t])
            if getattr(cfg, "bpart", 9) <= 1:
                return
            qTh = new([NT], BF16)
            qTh_t = stile("gqT")
            kTh = new([NT], BF16)
            kTh_t = stile("gkT")
            ktok = new([12, 64], BF16)
            ktok_t = stile("gktok")
            vtok = new([12, 128], BF16)
            vtok_t = stile("gvtok")
            osum = new([NT], F32)
            osum_t = stile("gosum")
            qt_ = new([2, 128], BF16)
            qt_t = stile("gqt", 2)
            kt_ = new([2, 128], BF16)
            kt_t = stile("gkt", 2)
            kdec = new([4, 64], BF16)
            kdec_t = stile("gkdec", 2)
            attT = new([2, 128], BF16)
            attT_t = stile("gatt", 2)
            elast = new([2, 2], F32)
            elast_t = stile("gel", 2)
            la = new([2, 64], F32)
            la_t = stile("gla", 2)
            lah = new([2, 64], BF16)
            lah_t = stile("glah", 2)
            lal = new([2, 64], BF16)
            lal_t = stile("glal", 2)
            t1 = new([2, 64], F32)
            t1_t = stile("gt1", 2)
            t2 = new([1, 128], F32)
            t2_t = stile("gt2", 1)
            Sf = new([2, 128], F32)
            Sf_t = stile("gSf", 2)
            Sb = new([2, 128], BF16)
            Sb_t = stile("gSb", 2)
            ks = gkey("gst")
            for h in range(4):
                wpiece(wpA, wpA_t, [(1536 + h * 64, 64), (1792 + h * 64, 64)])
                wpiece(wpB, wpB_t, [(2048 + h * 128, 128), (2560 + h * 128, 128)])
                for tb in range(3):
                    t0, n, _ = TB[tb]
                    bk = BK()
                    proj_feat(wpA, wpA_t, 0, 64, tb, bk)
                    act(qTh[0:64, t0:t0 + n], bk[0][0:64, 0:n], AF.Identity, [bk[1]], [qTh_t], scale=0.125)
                    bk = BK()
                    proj_feat(wpA, wpA_t, 64, 64, tb, bk)
                    vcp(kTh[0:64, t0:t0 + n], bk[0][0:64, 0:n], [bk[1]], [kTh_t])
                for ti in range(12):
                    bk, bt = BK()
                    proj_tok(wpA, wpA_t, 64, 64, ti * 128, bk[:, 0:64], bt)
                    proj_tok(wpB, wpB_t, 0, 128, ti * 128, bk[:, 128:256], bt)
                    vcp(ktok[:, ti, :], bk[:, 0:64], [bt], [ktok_t])
                    act(vtok[:, ti, :], bk[:, 128:256], AF.Copy, [bt], [vtok_t])
                for d in range(2):
                    cum = CMB("Ublk") if d == 0 else CMB("Lblk")
                    msk = CM("M_ge") if d == 0 else CM("M_le")
                    pb = d * 32

                    def pre(ti, i, d=d, cum=cum, msk=msk, pb=pb, h=h):
                        tok = slice(ti * 128, (ti + 1) * 128)
                        bk, bt = BK()
                        mm(bk[:, 0:64], gfa[0:64, tok], gup[0:64, d * 256 + h * 64:d * 256 + (h + 1) * 64], [gfa_t, gup_t], [bt])
                        act(la[:, i, :], bk[:, 0:64], AF.Exp, [bt], [la_t[i]], scale=-1.0)
                        act(la[:, i, :], la[:, i, :], AF.Ln, [la_t[i], cvec_t], [la_t[i]], bias=ONE)
                        vts(la[:, i, :], la[:, i, :], -1.0 / 16.0, None, ALU.mult, None, [la_t[i]], [la_t[i]])
                        act(lah[:, i, :], la[:, i, :], AF.Copy, [la_t[i]], [lah_t[i]])
                        vtt(lal[:, i, :], la[:, i, :], lah[:, i, :], ALU.subtract, [la_t[i], lah_t[i]], [lal_t[i]])
                        bk, bt = BK()
                        mm(bk[:, 0:64], cum, lah[:, i, :], [cmb_t, lah_t[i]], [bt], start=True, stop=False)
                        mm(bk[:, 0:64], cum, lal[:, i, :], [cmb_t, lal_t[i]], [bt], start=False, stop=True)
                        mm(bk[:, 64:128], CMB("Bones"), lah[:, i, :], [cmb_t, lah_t[i]], [bt], start=True, stop=False)
                        mm(bk[:, 64:128], CMB("Bones"), lal[:, i, :], [cmb_t, lal_t[i]], [bt], start=False, stop=True)
                        act(t1[:, i, :], bk[:, 0:64], AF.Copy, [bt], [t1_t[i]])
                        vtt(t1[:, i, :], bk[:, 64:128], t1[:, i, :], ALU.subtract, [bt, t1_t[i]], [t1_t[i]])
                        act(t1[:, i, :], t1[:, i, :], AF.Exp, [t1_t[i]], [t1_t[i]])
                        vtt(t1[:, i, :], ktok[:, ti, :], t1[:, i, :], ALU.mult, [ktok_t, t1_t[i]], [t1_t[i]])
                        for cc_ in range(2):
                            vts(kdec[:, i * 2 + cc_, :], t1[:, i, :], cvec[:, 2 + cc_:3 + cc_], None, ALU.mult, None,
                                [t1_t[i], cvec_t], [kdec_t[i]])
                        bk, bt = BK()
                        mm(bk[0:64, 0:128], lah[:, i, :], cum, [cmb_t, lah_t[i]], [bt], start=True, stop=False)
                        mm(bk[0:64, 0:128], lal[:, i, :], cum, [cmb_t, lal_t[i]], [bt], start=False, stop=True)
                        act(t2[0:64, 0, :], bk[0:64, 0:128], AF.Exp, [bt], [t2_t[0]])
                        vtt(qt_[0:64, i, :], qTh[0:64, tok], t2[0:64, 0, :], ALU.mult, [qTh_t, t2_t[0]], [qt_t[i]])
                        lc = 63 if d == 0 else 0
                        vcp(elast[0:64, i, :], t2[0:64, 0, :].rearrange("p (c t) -> p c t", c=2)[:, :, lc], [t2_t[0]], [elast_t[i]])
                        act(t2[0:64, 0, :], bk[0:64, 0:128], AF.Exp, [bt, t2_t[0]], [t2_t[0]], scale=-1.0)
                        vtt(kt_[0:64, i, :], kTh[0:64, tok], t2[0:64, 0, :], ALU.mult, [kTh_t, t2_t[0]], [kt_t[i]])
                        bk, bt = BK()
                        mm(bk[:, 0:128], kt_[0:64, i, :], qt_[0:64, i, :], [kt_t[i], qt_t[i]], [bt])
                        vtt(attT[:, i, :], bk[:, 0:128], msk, ALU.mult, [bt, cm_t], [attT_t[i]])

                    def step_fn(ti, cc, i, ci, d=d):
                        pr = slice(cc * 64, cc * 64 + 64)
                        t0 = ti * 128 + cc * 64
                        bo, bot = BK()
                        mm(bo[:, 0:64], Sb[0:64, ci, :], qt_[0:64, i, cc * 64:cc * 64 + 64], [Sb_t[ci], qt_t[i]], [bot],
                           start=True, stop=False)
                        mm(bo[:, 0:64], vtok[:, ti, :], attT[:, i, cc * 64:cc * 64 + 64], [vtok_t, attT_t[i]], [bot],
                           start=False, stop=True)
                        if d == 0:
                            vcp(osum[:, t0:t0 + 64], bo[:, 0:64], [bot], [osum_t])
                        else:
                            vtt(osum[:, t0:t0 + 64], bo[:, 0:64], osum[:, t0:t0 + 64], ALU.add, [bot, osum_t], [osum_t])
                        bs, bst = BK()
                        mm(bs[0:64, 0:128], kdec[:, i * 2 + cc, :], vtok[:, ti, :], [kdec_t[i], vtok_t], [bst])
                        vstt(Sf[0:64, ci, :], Sf[0:64, ci, :], elast[0:64, i, cc:cc + 1], bs[0:64, 0:128], ALU.mult, ALU.add,
                             [Sf_t[ci], elast_t[i], bst], [Sf_t[ci]])
                        vcp(Sb[0:64, ci, :], Sf[0:64, ci, :], [Sf_t[ci]], [Sb_t[ci]])

                    run_chains(d, h, 64, Sf, Sf_t, Sb, Sb_t, s_gla, o_gla, ks, pre, step_fn)
                if getattr(cfg, "bpart", 9) <= 2:
                    continue
                for tb, (t0, n, _) in enumerate(TB):
                    bk = BK()
                    proj_feat(wpB, wpB_t, 128, 128, tb, bk)
                    act(qTh[:, t0:t0 + n], bk[0][:, 0:n], AF.Silu, [bk[1]], [qTh_t])
                    fnorm(osum[:, t0:t0 + n], osum_t, n, osum[:, t0:t0 + n], osum_t, sv(4))
                    vtt(mg[0][:, t0:t0 + n], osum[:, t0:t0 + n], qTh[:, t0:t0 + n], ALU.mult, [osum_t, qTh_t], [mg_t[0]])
                out_proj(512 + h * 128, mg[0], mg_t[0])

        if "B" in cfg.mix:
            gla_mixer()
        dump("mgB_%d" % l, mg[0], [mg_t[0]])

        def gdn_mixer():
            apos[0] = base_mark
            wpA = new([KC, 16], BF16)
            wpA_t = stile("wpAD")
            wpB = new([KC, 128], BF16)
            wpB_t = stile("wpBD")
            beta = new([12, 8], F32)
            beta_t = stile("dbeta")
            gg = new([12, 8], F32)
            gg_t = stile("dg")
            negA = new([8], F32)
            negA_t = stile("dnegA")
            act(negA, svec[:, 14:22], AF.Exp, [svec_t], [negA_t])
            vts(negA, negA, -1.0, None, ALU.mult, None, [negA_t], [negA_t])
            gt = new([2, 8], F32)
            gt_t = stile("dgt", 2)
            gh = new([12, 8], BF16)
            gh_t = stile("dgh")
            gl = new([12, 8], BF16)
            gl_t = stile("dgl")
            ball = new([12, 8], F32)
            ball_t = stile("dball")
            tall = new([12, 8], F32)
            tall_t = stile("dtall")
            wpiece(wpA, wpA_t, [(6176, 16)])
            for ti in range(12):
                i = ti % 2
                bk, bt = BK()
                proj_tok(wpA, wpA_t, 0, 16, ti * 128, bk[:, 0:16], bt)
                act(gt[:, i, :], bk[:, 0:8], AF.Exp, [bt], [gt_t[i]], scale=-1.0)
                vts(gt[:, i, :], gt[:, i, :], 1.0, None, ALU.add, None, [gt_t[i]], [gt_t[i]])
                vrecip(beta[:, ti, :], gt[:, i, :], [gt_t[i]], [beta_t])
                vtt(gt[:, i, :], bk[:, 8:16], svec[:, 6:14], ALU.add, [bt, svec_t, gt_t[i]], [gt_t[i]])
                act(gt[:, i, :], gt[:, i, :], AF.Exp, [gt_t[i]], [gt_t[i]])
                act(gt[:, i, :], gt[:, i, :], AF.Ln, [gt_t[i], cvec_t], [gt_t[i]], bias=ONE)
                vtt(gg[:, ti, :], gt[:, i, :], negA, ALU.mult, [gt_t[i], negA_t], [gg_t])
                act(gh[:, ti, :], gg[:, ti, :], AF.Copy, [gg_t], [gh_t])
                vtt(gl[:, ti, :], gg[:, ti, :], gh[:, ti, :], ALU.subtract, [gg_t, gh_t], [gl_t])
                bk, bt = BK()
                for d in range(2):
                    cb = CMB("Ublk") if d == 0 else CMB("Lblk")
                    cs4 = slice(d * 4, d * 4 + 4)
                    mm(bk[:, d * 4:d * 4 + 4], cb, gh[:, ti, cs4], [cmb_t, gh_t], [bt], start=True, stop=False)
                    mm(bk[:, d * 4:d * 4 + 4], cb, gl[:, ti, cs4], [cmb_t, gl_t], [bt], start=False, stop=True)
                    mm(bk[:, 8 + d * 4:12 + d * 4], CMB("Bones"), gh[:, ti, cs4], [cmb_t, gh_t], [bt], start=True, stop=False)
                    mm(bk[:, 8 + d * 4:12 + d * 4], CMB("Bones"), gl[:, ti, cs4], [cmb_t, gl_t], [bt], start=False, stop=True)
                vcp(ball[:, ti, :], bk[:, 0:8], [bt], [ball_t])
                vcp(tall[:, ti, :], bk[:, 8:16], [bt], [tall_t])
            if getattr(cfg, "bpart", 9) <= 1:
                return
            qh = new([NT], BF16)
            qh_t = stile("dqh")
            kh = new([NT], BF16)
            kh_t = stile("dkh")
            ktok = new([12, 128], BF16)
            ktok_t = stile("dktok")
            vtok = new([12, 128], BF16)
            vtok_t = stile("dvtok")
            nft = new([512], F32)
            nft_t = stile("dnft")
            cols = new([2, 8], F32)
            cols_t = stile("dcols", 2)
            vnew = new([2, 128], BF16)
            vnew_t = stile("dvnew", 2)
            Sf = new([2, 128], F32)
            Sf_t = stile("dSf", 2)
            Sb = new([2, 128], BF16)
            Sb_t = stile("dSb", 2)
            ks = gkey("dst")
            PADN = 1548
            r1_mark = apos[0]
            CB = [(0, 0, 256), (256, 260, 256), (512, 520, 512), (1024, 1032, 512)]

            for h in range(4):
                apos[0] = r1_mark
                xpad = new([PADN], F32)
                xpad_t = stile("dxpad")
                acc = new([PADN], F32)
                acc_t = stile("dacc")
                S.op("dve", lambda e, xpad=xpad: e.memset(xpad, 0.0), w=[xpad_t])

                def conv_chunk(c0, chunk_idx):
                    wpiece(wpB, wpB_t, [(c0, 128)])
                    for tb in range(3):
                        bk = BK()
                        proj_feat(wpB, wpB_t, 0, 128, tb, bk)
                        if tb == 0:
                            vcp(xpad[:, 2:258], bk[0][:, 0:256], [bk[1]], [xpad_t])
                            vcp(xpad[:, 262:518], bk[0][:, 256:512], [bk[1]], [xpad_t])
                        else:
                            o = 522 + (tb - 1) * 512
                            vcp(xpad[:, o:o + 512], bk[0][:, 0:512], [bk[1]], [xpad_t])
                    NO = PADN - 4
                    wrow = convw[:, chunk_idx, :]
                    vts(acc[:, 0:NO], xpad[:, 0:NO], wrow[:, 0:1], None, ALU.mult, None, [xpad_t, convw_t], [acc_t])
                    for j in range(1, 5):
                        vstt(acc[:, 0:NO], xpad[:, j:j + NO], wrow[:, j:j + 1], acc[:, 0:NO], ALU.mult, ALU.add,
                             [xpad_t, convw_t, acc_t], [acc_t])
                    act(acc[:, 0:NO], acc[:, 0:NO], AF.Silu, [acc_t], [acc_t])

                conv_chunk(4128 + h * 128, h)
                for (t0, ao, n) in CB:
                    fnorm(acc[:, ao:ao + n], acc_t, n, qh[:, t0:t0 + n], qh_t, 128.0 ** -0.5, mean=False)
                conv_chunk(4128 + 512 + h * 128, 4 + h)
                for (t0, ao, n) in CB:
                    fnorm(acc[:, ao:ao + n], acc_t, n, nft[:, 0:n], nft_t, 1.0, mean=False)
                    vcp(kh[:, t0:t0 + n], nft[:, 0:n], [nft_t], [kh_t])
                    for s4 in range(n // 128):
                        bk, bt = BK()
                        tr(bk[:, 0:128], nft[:, s4 * 128:(s4 + 1) * 128], [nft_t], [bt])
                        vcp(ktok[:, t0 // 128 + s4, :], bk[:, 0:128], [bt], [ktok_t])
                conv_chunk(4128 + 1024 + h * 128, 8 + h)
                for (t0, ao, n) in CB:
                    for s4 in range(n // 128):
                        bk, bt = BK()
                        tr(bk[:, 0:128], acc[:, ao + s4 * 128:ao + (s4 + 1) * 128], [acc_t], [bt])
                        vcp(vtok[:, t0 // 128 + s4, :], bk[:, 0:128], [bt], [vtok_t])
                apos[0] = r1_mark
                osum = new([NT], F32)
                osum_t = stile("dosum")
                M = {}
                for nm in ("Dkq", "Dqk", "A", "ebb"):
                    M[nm] = (new([128], F32), stile("d" + nm))
                XH = new([6, 128], BF16)
                XH_t = stile("dXH", 6)
                XL = new([6, 128], BF16)
                XL_t = stile("dXL", 6)
                GrH = new([128], BF16)
                GrH_t = stile("dGrH")
                GrL = new([128], BF16)
                GrL_t = stile("dGrL")
                kbe = new([128], BF16)
                kbe_t = stile("dkbe")
                vb = new([128], BF16)
                vb_t = stile("dvb")
                utok = new([2, 128], BF16)
                utok_t = stile("dutok", 2)
                wT = new([2, 128], BF16)
                wT_t = stile("dwT", 2)
                attT = new([2, 128], BF16)
                attT_t = stile("datt", 2)
                qt_ = new([2, 128], BF16)
                qt_t = stile("dqt", 2)
                kdec = new([4, 128], BF16)
                kdec_t = stile("dkdec", 2)
                elast = new([2, 2], F32)
                elast_t = stile("del", 2)
                for d in range(2):
                    cum = CMB("Ublk") if d == 0 else CMB("Lblk")
                    nm_kq = CM("NM_ge") if d == 0 else CM("NM_le")
                    nm_qk = CM("NM_lt") if d == 0 else CM("NM_gt")
                    gc = d * 4 + h

                    def pre(ti, i, d=d, cum=cum, nm_kq=nm_kq, nm_qk=nm_qk, gc=gc, M=M, GrH=GrH, GrH_t=GrH_t, GrL=GrL,
                            GrL_t=GrL_t, XH=XH, XH_t=XH_t, XL=XL, XL_t=XL_t, kbe=kbe,
                            kbe_t=kbe_t, vb=vb, vb_t=vb_t, utok=utok, utok_t=utok_t, wT=wT, wT_t=wT_t, attT=attT,
                            attT_t=attT_t, qt_=qt_, qt_t=qt_t, kdec=kdec, kdec_t=kdec_t, elast=elast, elast_t=elast_t):
                        tok = slice(ti * 128, (ti + 1) * 128)
                        gcol = gg[:, ti, gc:gc + 1]
                        bcol_ = beta[:, ti, gc:gc + 1]
                        co = cols[:, i, :]
                        co_t = cols_t[i]
                        vcp(GrH, gh[:, ti, gc:gc + 1].to_broadcast([128, 128]), [gh_t], [GrH_t])
                        vcp(GrL, gl[:, ti, gc:gc + 1].to_broadcast([128, 128]), [gl_t], [GrL_t])
                        bb, bbt = BK()
                        mm(bb[:, 0:128], GrH, cum, [GrH_t, cmb_t], [bbt], start=True, stop=False)
                        mm(bb[:, 0:128], GrL, cum, [GrL_t, cmb_t], [bbt], start=False, stop=True)
                        vcp(co[:, 0:1], ball[:, ti, gc:gc + 1], [ball_t], [co_t])
                        vcp(co[:, 1:2], tall[:, ti, gc:gc + 1], [tall_t], [co_t])
                        vts(co[:, 2:3], co[:, 0:1], -1.0, None, ALU.mult, None, [co_t], [co_t])
                        vts(co[:, 5:6], bcol_, -1.0, None, ALU.mult, None, [beta_t, co_t], [co_t])
                        act(co[:, 3:4], co[:, 0:1], AF.Exp, [co_t], [co_t])
                        vtt(co[:, 3:4], co[:, 3:4], bcol_, ALU.mult, [co_t, beta_t], [co_t])
                        vtt(co[:, 4:5], co[:, 1:2], co[:, 0:1], ALU.subtract, [co_t], [co_t])
                        act(co[:, 4:5], co[:, 4:5], AF.Exp, [co_t], [co_t])
                        Dkq, Dkq_t = M["Dkq"]
                        Dqk, Dqk_t = M["Dqk"]
                        vtt(Dkq, bb[:, 0:128], nm_kq, ALU.add, [bbt, cm_t], [Dkq_t])
                        act(Dkq, Dkq, AF.Exp, [Dkq_t, co_t], [Dkq_t], bias=co[:, 2:3])
                        vstt(Dqk, bb[:, 0:128], -1.0, nm_qk, ALU.mult, ALU.add, [bbt, cm_t], [Dqk_t])
                        act(Dqk, Dqk, AF.Exp, [Dqk_t, co_t], [Dqk_t], bias=co[:, 0:1])
                        ebb, ebb_t = M["ebb"]
                        act(ebb, bb[:, 0:128], AF.Exp, [bbt], [ebb_t])
                        vtt(qt_[:, i, :], qh[:, tok], ebb, ALU.mult, [qh_t, ebb_t], [qt_t[i]])
                        lc = 63 if d == 0 else 0
                        vcp(elast[:, i, :], ebb.rearrange("p (c t) -> p c t", c=2)[:, :, lc], [ebb_t], [elast_t[i]])
                        A0, A0_t = M["A"]

                        def split(src, src_tiles, k):
                            vcp(XH[:, k, :], src, src_tiles, [XH_t[k]])
                            vtt(XL[:, k, :], src, XH[:, k, :], ALU.subtract, list(src_tiles) + [XH_t[k]], [XL_t[k]])

                        def mm3(out, bt_, a, b, first=True, last=True):
                            mm(out, XH[:, a, :], XH[:, b, :], [XH_t[a], XH_t[b]], [bt_], start=first, stop=False)
                            mm(out, XH[:, a, :], XL[:, b, :], [XH_t[a], XL_t[b]], [bt_], start=False, stop=False)
                            mm(out, XL[:, a, :], XH[:, b, :], [XL_t[a], XH_t[b]], [bt_], start=False, stop=last)

                        bk, bt = BK()
                        mm(bk[:, 0:128], kh[:, tok], kh[:, tok], [kh_t], [bt])
                        vstt(A0, bk[:, 0:128], co[:, 5:6], Dqk, ALU.mult, ALU.mult, [bt, co_t, Dqk_t], [A0_t])
                        split(A0, [A0_t], 0)
                        bk, bt = BK()
                        tr(bk[:, 0:128], A0, [A0_t], [bt])
                        split(bk[:, 0:128], [bt], 2)
                        vtt(A0, bk[:, 0:128], CM("ident"), ALU.add, [bt, cm_t, A0_t], [A0_t])
                        split(A0, [A0_t], 4)
                        ca, cq = 0, 0
                        for it in range(1, 1 + getattr(cfg, "neu", 5)):
                            na = 1 - ca
                            bk, bt = BK()
                            mm3(bk[:, 0:128], bt, 2 + ca, ca)
                            if it < 5:
                                mm3(bk[:, 128:256], bt, ca, 2 + ca)
                            split(bk[:, 0:128], [bt], na)
                            if it < 5:
                                split(bk[:, 128:256], [bt], 2 + na)
                            bq, bqt = BK()
                            mm(bq[:, 0:128], CMB("ident"), XH[:, 4 + cq, :], [cmb_t, XH_t[4 + cq]], [bqt], start=True, stop=False)
                            mm(bq[:, 0:128], CMB("ident"), XL[:, 4 + cq, :], [cmb_t, XL_t[4 + cq]], [bqt], start=False, stop=False)
                            mm3(bq[:, 0:128], bqt, na, 4 + cq, first=False, last=True)
                            split(bq[:, 0:128], [bqt], 4 + (1 - cq))
                            ca, cq = na, 1 - cq
                        TTb = XH[:, 4 + cq, :]
                        TTb_t = XH_t[4 + cq]
                        vts(kbe, ktok[:, ti, :], co[:, 3:4], None, ALU.mult, None, [ktok_t, co_t], [kbe_t])
                        vts(vb, vtok[:, ti, :], bcol_, None, ALU.mult, None, [vtok_t, beta_t], [vb_t])
                        bk, bt = BK()
                        mm(bk[:, 0:128], TTb, vb, [TTb_t, vb_t], [bt])
                        mm(bk[:, 128:256], kbe, TTb, [TTb_t, kbe_t], [bt])
                        vcp(utok[:, i, :], bk[:, 0:128], [bt], [utok_t[i]])
                        act(wT[:, i, :], bk[:, 128:256], AF.Copy, [bt], [wT_t[i]])
                        bk, bt = BK()
                        mm(bk[:, 0:128], kh[:, tok], qh[:, tok], [kh_t, qh_t], [bt])
                        vtt(attT[:, i, :], bk[:, 0:128], Dkq, ALU.mult, [bt, Dkq_t], [attT_t[i]])
                        for cc_ in range(2):
                            vts(kdec[:, i * 2 + cc_, :], ktok[:, ti, :], co[:, 4:5], cvec[:, 2 + cc_:3 + cc_], ALU.mult, ALU.mult,
                                [ktok_t, co_t, cvec_t], [kdec_t[i]])

                    def step_fn(ti, cc, i, ci, d=d, utok=utok, utok_t=utok_t, wT=wT, wT_t=wT_t, attT=attT, attT_t=attT_t,
                                qt_=qt_, qt_t=qt_t, kdec=kdec, kdec_t=kdec_t, elast=elast, elast_t=elast_t, osum=osum,
                                osum_t=osum_t):
                        pr = slice(cc * 64, cc * 64 + 64)
                        t0 = ti * 128 + cc * 64
                        bw, bwt = BK()
                        mm(bw[:, 0:128], wT[:, i, :], Sb[:, ci, :], [wT_t[i], Sb_t[ci]], [bwt])
                        vtt(vnew[:, cc, :], utok[:, i, :], bw[:, 0:128], ALU.subtract, [utok_t[i], bwt], [vnew_t[cc]])
                        bo, bot = BK()
                        mm(bo[:, 0:64], Sb[:, ci, :], qt_[:, i, cc * 64:cc * 64 + 64], [Sb_t[ci], qt_t[i]], [bot],
                           start=True, stop=False)
                        mm(bo[:, 0:64], vnew[:, cc, :], attT[:, i, cc * 64:cc * 64 + 64], [vnew_t[cc], attT_t[i]], [bot],
                           start=False, stop=True)
                        if d == 0:
                            vcp(osum[:, t0:t0 + 64], bo[:, 0:64], [bot], [osum_t])
                        else:
                            vtt(osum[:, t0:t0 + 64], bo[:, 0:64], osum[:, t0:t0 + 64], ALU.add, [bot, osum_t], [osum_t])
                        bs, bst = BK()
                        mm(bs[:, 0:128], kdec[:, i * 2 + cc, :], vnew[:, cc, :], [kdec_t[i], vnew_t[cc]], [bst])
                        vstt(Sf[:, ci, :], Sf[:, ci, :], elast[:, i, cc:cc + 1], bs[:, 0:128], ALU.mult, ALU.add,
                             [Sf_t[ci], elast_t[i], bst], [Sf_t[ci]])
                        vcp(Sb[:, ci, :], Sf[:, ci, :], [Sf_t[ci]], [Sb_t[ci]])

                    run_chains(d, h, 128, Sf, Sf_t, Sb, Sb_t, s_gdn, o_gdn, ks, pre, step_fn)
                wpiece(wpB, wpB_t, [(5664 + h * 128, 128)])
                for tb, (t0, n, _) in enumerate(TB):
                    bk = BK()
                    proj_feat(wpB, wpB_t, 0, 128, tb, bk)
                    act(qh[:, t0:t0 + n], bk[0][:, 0:n], AF.Silu, [bk[1]], [qh_t])
                    fnorm(osum[:, t0:t0 + n], osum_t, n, osum[:, t0:t0 + n], osum_t, sv(5))
                    vtt(mg[0][:, t0:t0 + n], osum[:, t0:t0 + n], qh[:, t0:t0 + n], ALU.mult, [osum_t, qh_t], [mg_t[0]])
                out_proj(1536 + h * 128, mg[0], mg_t[0])

        if "D" in cfg.mix:
            gdn_mixer()
        dump("mgD_%d" % l, mg[0], [mg_t[0]])


    def store_y():
        scratch_reset()
        yb = [new([D], F32) for _ in range(2)]
        yb_t = S.tiles("yb", 2, 4, scr=True)
        yk = [S.key("yb0"), S.key("yb1")]
        out_keys.extend(yk)
        ev = 0
        for ti in range(NT // 128):
            b = ti % 2
            tt = ti // 4
            for q in range(4):
                bk = next_bank()
                for j in range(4):
                    dc = q * 4 + j
                    S.op("pe", lambda e, bk=bk, j=j, dc=dc, ti=ti: e.transpose(
                        banks[bk][:, j * 128:(j + 1) * 128], xT[:, dc, ti * 128:(ti + 1) * 128], ident),
                        r=[xT_t[dc][tt], ident_t], w=[bankT[bk]])
                eng = "dve" if ev % 2 == 0 else "act"
                ev += 1
                dst = yb[b][:, q * 512:(q + 1) * 512]
                if eng == "dve":
                    S.op("dve", lambda e, dst=dst, bk=bk: e.tensor_copy(dst, banks[bk][:, :]),
                         r=[bankT[bk]], w=[yb_t[b][q]])
                else:
                    S.op("act", lambda e, dst=dst, bk=bk: e.activation(dst, banks[bk][:, :], AF.Copy),
                         r=[bankT[bk]], w=[yb_t[b][q]])
            S.op("sp", lambda e, b=b, ti=ti: e.dma_start(out=y[ti * 128:(ti + 1) * 128, :], in_=yb[b]),
                 r=yb_t[b], key=yk[b])

    load_x()
    dump("xT0", xT, alltiles(xT_t))
    for l in range(L):
        adaln(l)
        if l == 0:
            dump("modT", modT, [mod_t])
            dump("AB", AB, [ab_t])
        norm_mod(0)
        if l == 0:
            dump("hT0", hT, alltiles(hT_t))
        ffn(l, 0, 2)
        if l == 0:
            dump("xT1", xT, alltiles(xT_t))
        if cfg.mixers:
            norm_mod(1)
            mixers(l)
        norm_mod(2)
        ffn(l, 1, 8)
    store_y()
    fin = S.op("sp", lambda e: e.nop(), r=[], w=[])
    for k in list(out_keys) + [S.gkey(n) for n in ("stg0", "stg1", "gst", "dst")]:
        fin.deps[("dma", k)] = k.count
    S.emit()
    st.close()
    return nc


def host_consts():
    f32 = np.float32
    idx = np.arange(128)
    p, f = idx[:, None], idx[None, :]
    same = (p // 64) == (f // 64)
    RT = np.zeros((128, 128), f32)
    for m in range(128):
        if m % 64 < 32:
            RT[m + 32, m] = -1.0
        else:
            RT[m - 32, m] = 1.0
    NEG = -30000.0
    mats = dict(
        ident=np.eye(128), ones128=np.full((128, 128), 1.0 / 128), ones1=np.ones((128, 128)), RT=RT,
        Ublk=(same & (p <= f)), Lblk=(same & (p >= f)), Bones=same,
        NM_le=np.where(same & (f <= p), 0.0, NEG), NM_lt=np.where(same & (f < p), 0.0, NEG),
        NM_ge=np.where(same & (f >= p), 0.0, NEG), NM_gt=np.where(same & (f > p), 0.0, NEG),
        M_ge=(same & (f >= p)), M_le=(same & (f <= p)))
    cm = np.stack([np.asarray(mats[n], f32) for n in CM_NAMES], axis=1)
    cmat = np.ascontiguousarray(cm).reshape(128, NCM * 128)
    t = np.arange(1024)
    row = (t // 64).astype(f32)
    col = (t % 64).astype(f32)
    inv = (np.float32(10000.0) ** (-np.arange(0, 64, 2, dtype=f32) / np.float32(64))).astype(f32)
    d = np.arange(128)
    pos = np.where((d // 64)[:, None] == 0, row[None, :], col[None, :]).astype(f32)
    ang = (pos * inv[d % 32][:, None]).astype(f32)
    sgn = np.where((d % 64) < 32, -1.0, 1.0).astype(f32)[:, None]
    rope = np.concatenate([np.cos(ang), np.sin(ang) * sgn], axis=1).astype(f32)
    return cmat, np.ascontiguousarray(rope)


def make_in_maps(cfg, inp, n_cores):
    L, KC = cfg.depth, cfg.KC
    f32 = np.float32
    normgT = np.ascontiguousarray(
        np.asarray(inp["norm_g"], f32).reshape(L, 3, KC, 128).transpose(3, 0, 1, 2)).reshape(128, L * 3 * KC)
    bm = np.asarray(inp["b_mod"], f32).reshape(L, 144, 128).transpose(2, 0, 1)
    bmodT = np.ascontiguousarray(np.repeat(bm[:, :, :, None], 2, axis=3)).reshape(128, L * 144 * 2)
    ident = np.eye(128, dtype=f32)
    shared = dict(normgT=normgT, bmodT=bmodT, ident=ident,
                  w_mod=np.asarray(inp["w_mod"], f32), ffn_gu=np.asarray(inp["ffn_gu"], f32),
                  ffn_down=np.asarray(inp["ffn_down"], f32))
    if cfg.mixers:
        cmat, rope = host_consts()
        sv = np.zeros((128, L, NSV), f32)
        sv[:, :, 0] = np.asarray(inp["na_qk_norm"], f32)[:, 0, :].T
        sv[:, :, 1] = np.asarray(inp["na_qk_norm"], f32)[:, 1, :].T
        sv[:, :, 2] = np.asarray(inp["gqa_qk_norm"], f32)[:, 0, :].T
        sv[:, :, 3] = np.asarray(inp["gqa_qk_norm"], f32)[:, 1, :].T
        sv[:, :, 4] = np.asarray(inp["gla_out_norm"], f32).T
        sv[:, :, 5] = np.asarray(inp["gdn_out_norm"], f32).T
        sv[:, :, 6:14] = np.asarray(inp["gdn_dt_bias"], f32).reshape(L, 8)[None]
        sv[:, :, 14:22] = np.asarray(inp["gdn_a_log"], f32).reshape(L, 8)[None]
        convw = np.ascontiguousarray(
            np.asarray(inp["gdn_conv"], f32).reshape(L, 5, 12, 128).transpose(3, 0, 2, 1)).reshape(128, L * 12 * 5)
        gup = np.zeros((64, L, 2, 256), f32)
        gu = np.asarray(inp["gla_gate_up"], f32)
        gb = np.asarray(inp["gla_gate_bias"], f32)
        for d in range(2):
            gup[d * 16:d * 16 + 16, :, d, :] = gu[:, d].transpose(1, 0, 2)
            gup[32, :, d, :] = gb[:, d]
        rpb = np.asarray(inp["na_rpb"], f32)
        ck = np.arange(64)[:, None]
        cq = np.arange(64)[None, :]
        c0 = np.clip(cq - 8, 0, 48)
        inwin = (ck >= c0) & (ck < c0 + 16)
        dc = np.clip(ck - cq + 15, 0, 30)
        rpbx = np.full((2, 64, L, 4, 14, 64), -30000.0, f32)
        for rs in range(2):
            for di in range(14):
                g = rpb[:, :, di + rs, :][:, :, dc]
                g = np.where(inwin[None, None], g, np.float32(-30000.0))
                rpbx[rs, :, :, :, di, :] = g.transpose(2, 0, 1, 3)
        shared.update(cmat=cmat, rope=rope, svec=np.ascontiguousarray(sv).reshape(128, L * NSV), convw=convw,
                      gup=np.ascontiguousarray(gup).reshape(64, L * 512),
                      rpbx=np.ascontiguousarray(rpbx).reshape(128, L * 4 * 14 * 64),
                      w_in=np.asarray(inp["w_in"], f32), w_out=np.asarray(inp["w_out"], f32))
    maps = []
    xp = np.asarray(inp["x_prompt"], f32)
    xs = np.asarray(inp["x_sample"], f32)
    c = np.asarray(inp["c"], f32)
    cctx = np.asarray(inp["c_ctx"], f32)
    for i in range(n_cores):
        m = dict(shared)
        m["xin"] = np.ascontiguousarray(np.concatenate([xp[2 * i], xp[2 * i + 1], xs[i]], axis=0))
        cc = np.stack([cctx, c[i]], axis=-1)
        m["condT"] = np.ascontiguousarray(cc.reshape(KC, 128, 2).transpose(1, 0, 2)).reshape(128, KC * 2)
        if cfg.mixers:
            m["c_nak"] = np.ascontiguousarray(np.asarray(inp["cache_na_k"], f32)[i])
            m["c_nav"] = np.ascontiguousarray(np.asarray(inp["cache_na_v"], f32)[i])
            m["c_gqk"] = np.ascontiguousarray(np.asarray(inp["cache_gqa_k"], f32)[i])
            m["c_gqv"] = np.ascontiguousarray(np.asarray(inp["cache_gqa_v"], f32)[i])
            m["s_gla"] = np.ascontiguousarray(np.asarray(inp["state_gla"], f32)[i])
            m["s_gdn"] = np.ascontiguousarray(np.asarray(inp["state_gdn"], f32)[i])
        maps.append(m)
    return maps


def run(cfg, inp, n_cores, trace=False):
    nc = build(cfg)
    maps = make_in_maps(cfg, inp, n_cores)
    res = run_bass_kernel_spmd(nc, maps, core_ids=list(range(n_cores)), trace=trace)
    r = res.results
    run.last = r
    TP = cfg.TP
    yp = np.stack([r[i]["y"][k * TP:(k + 1) * TP] for i in range(n_cores) for k in range(2)], axis=0)
    ys = np.stack([r[i]["y"][2 * TP:] for i in range(n_cores)], axis=0)
    outs = [yp, ys]
    if cfg.mixers:
        for nm in ("o_nak", "o_nav", "o_gqk", "o_gqv", "o_gla", "o_gdn"):
            outs.append(np.concatenate([r[i][nm] for i in range(n_cores)], axis=0))
    return tuple(outs), res


def kernel(**inputs):
    cfg = Cfg()
    outs, _ = run(cfg, inputs, 8)
    return outs
```

```python
import numpy as np
import concourse.bass as bass
import concourse.mybir as mybir
from concourse.bass_utils import run_bass_kernel_spmd

F32 = mybir.dt.float32
BF16 = mybir.dt.bfloat16
AF = mybir.ActivationFunctionType
ALU = mybir.AluOpType
AX = mybir.AxisListType

SAME_ENGINE_SYNC = True
CM_NAMES = ["ident", "ones128", "ones1", "RT", "Ublk", "Lblk", "Bones", "NM_le", "NM_lt", "NM_ge", "NM_gt", "M_ge", "M_le"]
NCM = len(CM_NAMES)
CMI = {n: i for i, n in enumerate(CM_NAMES)}
NSV = 6 + 16


class T:
    __slots__ = ("name", "lw", "rd", "scr", "excl")

    def __init__(self, name, scr=False, init=None):
        self.name = name
        self.lw = {}
        self.rd = dict(init) if init else {}
        self.scr = scr
        self.excl = False


class SemKey:
    def __init__(self, name):
        self.name = name
        self.count = 0
        self.sem = None


class Op:
    __slots__ = ("eng", "fn", "deps", "signal", "count", "key", "idx")


class Sched:
    ENGS = ("pe", "act", "dve", "pool", "sp")

    def __init__(self, nc):
        self.nc = nc
        self.ops = {e: [] for e in self.ENGS}
        self.keys = []
        self.scr_marks = {}

    def key(self, name):
        k = SemKey(name)
        self.keys.append(k)
        return k

    def gkey(self, name):
        if not hasattr(self, "_gk"):
            self._gk = {}
        if name not in self._gk:
            self._gk[name] = self.key(name)
        return self._gk[name]

    def tiles(self, name, *dims, scr=False):
        if not dims:
            return T(name, scr, self.scr_marks if scr else None)
        return [self.tiles("%s_%d" % (name, i), *dims[1:], scr=scr) for i in range(dims[0])]

    def _add(self, deps, m, eng):
        if m is None:
            return
        if m[0] == "dma":
            k = m[1]
            deps[("dma", k)] = max(deps.get(("dma", k), 0), k.count)
        else:
            e, idx = m[1], m[2]
            if e == eng and (eng == "pe" or not SAME_ENGINE_SYNC):
                return
            deps[("eng", e)] = max(deps.get(("eng", e), -1), idx)

    def op(self, eng, fn, r=(), w=(), key=None, samesync=True):
        o = Op()
        o.eng, o.fn, o.signal, o.count, o.key = eng, fn, False, 0, key
        o.idx = len(self.ops[eng])
        deps = {}
        for t in r:
            for m in t.lw.values():
                self._add(deps, m, eng)
            if t.excl:
                for m in t.rd.values():
                    if m[0] == "eng" and m[1] != eng:
                        self._add(deps, m, eng)
        for t in w:
            for m in t.lw.values():
                self._add(deps, m, eng)
            for m in t.rd.values():
                self._add(deps, m, eng)
        o.deps = deps
        if key is not None:
            key.count += 16
            m = ("dma", key, key.count)
            mk = ("dma", key)
        else:
            m = ("eng", eng, o.idx)
            mk = ("eng", eng)
        for t in w:
            t.lw[mk] = m
            t.rd = {}
            if t.scr:
                self.scr_marks[mk] = m
        for t in r:
            t.rd[mk] = m
            if t.scr:
                self.scr_marks[mk] = m
        self.ops[eng].append(o)
        return o

    def emit(self):
        nc = self.nc
        from contextlib import ExitStack
        for e in self.ENGS:
            for o in self.ops[e]:
                for (kind, k), v in o.deps.items():
                    if kind == "eng":
                        self.ops[k][v].signal = True
        for e in self.ENGS:
            c = 0
            for o in self.ops[e]:
                if o.signal and o.key is None:
                    c += 1
                o.count = c
        with ExitStack() as st:
            engsem = {e: st.enter_context(nc.semaphore("sem_" + e)) for e in self.ENGS}
            for k in self.keys:
                k.sem = st.enter_context(nc.semaphore("k_" + k.name))
            block = st.enter_context(nc.Block())
            bname = {"pe": "tensor", "act": "scalar", "dve": "vector", "pool": "gpsimd", "sp": "sync"}
            for eng in self.ENGS:
                ops = self.ops[eng]
                if not ops:
                    continue

                def body(e, ops=ops):
                    known = {}
                    for o in ops:
                        for (kind, k), v in o.deps.items():
                            if kind == "eng":
                                sem, val, kid = engsem[k], self.ops[k][v].count, k
                            else:
                                sem, val, kid = k.sem, v, k.name
                            if known.get(kid, 0) < val:
                                e.wait_ge(sem, val)
                                known[kid] = val
                        ins = o.fn(e)
                        if o.key is not None:
                            ins.then_inc(o.key.sem, 16)
                        elif o.signal:
                            ins.then_inc(engsem[o.eng], 1)

                getattr(block, bname[eng])(body)


class Cfg:
    def __init__(self, depth=4, F=5504, mixers=True):
        self.depth = depth
        self.D = 2048
        self.KC = 16
        self.F = F
        self.FC = F // 128
        self.NP = 2
        self.TP = 256
        self.TS = 1024
        self.NT = self.NP * self.TP + self.TS
        self.NTT = self.NT // 512
        self.IN_COLS = 6192
        self.mixers = mixers
        self.G = 4
        self.mix = "ABCD"


def tt_cond(tt):
    return 0 if tt == 0 else 1


def build(cfg):
    nc = bass.Bass("TRN2", target_bir_lowering=False)
    D, KC, NT, NTT, L, FC = cfg.D, cfg.KC, cfg.NT, cfg.NTT, cfg.depth, cfg.FC

    def din(name, shape, dt=F32):
        return nc.dram_tensor(name, list(shape), dt, kind="ExternalInput").ap()

    def dout(name, shape, dt=F32):
        return nc.dram_tensor(name, list(shape), dt, kind="ExternalOutput").ap()

    xin = din("xin", [NT, D])
    condT = din("condT", [128, KC * 2])
    normgT = din("normgT", [128, L * 3 * KC])
    bmodT = din("bmodT", [128, L * 144 * 2])
    ident_d = din("ident", [128, 128])
    w_mod = din("w_mod", [L, D, 9 * D])
    ffn_gu = din("ffn_gu", [L, 2, D, 2 * cfg.F])
    ffn_down = din("ffn_down", [L, 2, cfg.F, D])
    y = dout("y", [NT, D])
    MX = cfg.mixers
    if MX:
        w_in = din("w_in", [L, D, 6192])
        w_out = din("w_out", [L, D, D])
        cmat_d = din("cmat", [128, NCM * 128])
        rope_d = din("rope", [128, 2 * 1024])
        svec_d = din("svec", [128, L * NSV])
        convw_d = din("convw", [128, L * 12 * 5])
        gup_d = din("gup", [64, L * 512])
        rpbx_d = din("rpbx", [128, L * 4 * 14 * 64])
        c_nak = din("c_nak", [L, 4, 256, 128])
        c_nav = din("c_nav", [L, 4, 256, 128])
        c_gqk = din("c_gqk", [L, 2, 256, 128])
        c_gqv = din("c_gqv", [L, 2, 256, 128])
        s_gla = din("s_gla", [L, 2, 4, 64, 128])
        s_gdn = din("s_gdn", [L, 2, 4, 128, 128])
        o_nak = dout("o_nak", [2, L, 4, 256, 128])
        o_nav = dout("o_nav", [2, L, 4, 256, 128])
        o_gqk = dout("o_gqk", [2, L, 2, 256, 128])
        o_gqv = dout("o_gqv", [2, L, 2, 256, 128])
        o_gla = dout("o_gla", [2, L, 2, 4, 64, 128])
        o_gdn = dout("o_gdn", [2, L, 2, 4, 128, 128])

    S = Sched(nc)
    from contextlib import ExitStack
    st = ExitStack()

    ARENA_BYTES = 212736
    arena = st.enter_context(nc.sbuf_tensor("arena", [128, ARENA_BYTES // 4], F32))
    apos = [0]

    def alloc(nbytes):
        off = apos[0]
        apos[0] += (nbytes + 31) // 32 * 32
        assert apos[0] <= ARENA_BYTES, "arena overflow %d" % apos[0]
        return off

    def view(off, nbytes, dt, pattern=None, **kw):
        a = arena[:, off // 4:(off + nbytes) // 4]
        if dt is not F32:
            a = a.bitcast(dt)
        if pattern:
            a = a.rearrange(pattern, **kw)
        return a

    def new(shape, dt):
        esz = 4 if dt is F32 else 2
        n = int(np.prod(shape)) * esz
        off = alloc(n)
        if len(shape) == 1:
            return view(off, n, dt)
        if len(shape) == 2:
            return view(off, n, dt, "p (a b) -> p a b", a=shape[0])
        return view(off, n, dt, "p (a b c) -> p a b c", a=shape[0], b=shape[1])

    xT = new([KC, NT], F32)
    hT = new([KC, NT], BF16)
    ident = new([128], F32) if not cfg.mixers else None
    normg = new([3, KC], F32)
    cT = new([KC, 2], F32)
    scT = new([KC, 2], BF16)
    modT = new([144, 2], F32)
    AB = new([9, KC, 2], F32)
    if MX:
        cmat = new([NCM, 128], F32)
        svec = new([NSV], F32)
        convw = new([12, 5], F32)
        cvec = new([4], F32)
        ident = cmat[:, CMI["ident"], :]
        ones_bf = new([128], BF16)
        cmat_bf = new([NCM, 128], BF16)
    scratch0 = apos[0]
    SCR = ARENA_BYTES - scratch0

    def scratch_reset():
        apos[0] = scratch0

    banks = [st.enter_context(nc.psum_tensor("ps%d" % i, [128, 512], F32)) for i in range(8)]
    bankT = S.tiles("bank", 8)
    for _bt in bankT:
        _bt.excl = True
    bank_rr = [0]

    def next_bank():
        b = bank_rr[0]
        bank_rr[0] = (b + 1) % 8
        return b

    xT_t = S.tiles("xT", KC, NTT)
    hT_t = S.tiles("hT", KC, NTT)
    const_t = S.tiles("const")
    mod_t = S.tiles("mod")
    ab_t = S.tiles("ab")

    kc_const = S.key("const")
    ident_t, normg_t, cT_t, scT_t, ones_t, eps_t = [S.tiles(n) for n in
                                                   ("ident", "normg", "cT", "scT", "ones", "eps")]
    if not cfg.mixers:
        S.op("sp", lambda e: e.dma_start(out=ident, in_=ident_d), w=[ident_t], key=S.gkey("const1"))
    S.op("sp", lambda e: e.dma_start(out=cT.rearrange("p a b -> p (a b)"), in_=condT), w=[cT_t], key=S.gkey("const3"))
    S.op("act", lambda e: e.activation(scT, cT, AF.Silu), r=[cT_t], w=[scT_t])

    def load_x():
        scratch_reset()
        xb = [new([D], F32) for _ in range(2)]
        xb_t = S.tiles("xb", 2, scr=True)
        xk = [S.key("xb0"), S.key("xb1")]
        ev = 0
        for ti in range(NT // 128):
            b = ti % 2
            S.op("sp", lambda e, b=b, ti=ti: e.dma_start(out=xb[b], in_=xin[ti * 128:(ti + 1) * 128, :]),
                 w=[xb_t[b]], key=xk[b])
            tt = ti // 4
            for q in range(4):
                bk = next_bank()
                for j in range(4):
                    dc = q * 4 + j
                    S.op("pe", lambda e, bk=bk, j=j, b=b, dc=dc: e.transpose(
                        banks[bk][:, j * 128:(j + 1) * 128], xb[b][:, dc * 128:(dc + 1) * 128], ident),
                        r=[xb_t[b], ident_t], w=[bankT[bk]])
                eng = "dve" if ev % 2 == 0 else "act"
                ev += 1
                dst = xT[:, q * 4:(q + 1) * 4, ti * 128:(ti + 1) * 128]
                src = banks[bk][:, :].rearrange("p (a b) -> p a b", a=4)
                if eng == "dve":
                    S.op("dve", lambda e, dst=dst, src=src: e.tensor_copy(dst, src),
                         r=[bankT[bk]], w=[xT_t[q * 4 + j][tt] for j in range(4)])
                else:
                    S.op("act", lambda e, dst=dst, src=src: e.activation(dst, src, AF.Copy),
                         r=[bankT[bk]], w=[xT_t[q * 4 + j][tt] for j in range(4)])

    def adaln(l):
        scratch_reset()
        NB = 512
        S.op("sp", lambda e: e.dma_start(out=normg.rearrange("p a b -> p (a b)"), in_=normgT[:, l * 3 * KC:(l + 1) * 3 * KC]),
             w=[normg_t], key=S.gkey("const2"))
        if MX:
            S.op("sp", lambda e: e.dma_start(out=svec, in_=svec_d[:, l * NSV:(l + 1) * NSV]), w=[svec_t], key=S.gkey("const5"))
            S.op("sp", lambda e: e.dma_start(out=convw.rearrange("p a b -> p (a b)"), in_=convw_d[:, l * 60:(l + 1) * 60]),
                 w=[convw_t], key=S.gkey("const6"))
        bmod = new([144, 2], F32)
        bmod_t = S.tiles("bmod", scr=True)
        S.op("sp", lambda e: e.dma_start(out=bmod.rearrange("p a b -> p (a b)"), in_=bmodT[:, l * 288:(l + 1) * 288]),
             w=[bmod_t], key=S.gkey("bmod"))
        wb = [new([KC, NB], BF16) for _ in range(2)]
        wb_t = S.tiles("wmb", 2, scr=True)
        wk = [S.gkey("wmb0"), S.gkey("wmb1")]
        bk = next_bank()
        nblk = 9 * D // NB
        for blk in range(nblk):
            b = blk % 2
            src = w_mod[l, :, blk * NB:(blk + 1) * NB].rearrange("(kc p) n -> p kc n", p=128)
            S.op("pool", lambda e, b=b, src=src: e.dma_start(out=wb[b], in_=src), w=[wb_t[b]], key=wk[b])
            for s in range(NB // 128):
                cb = blk * (NB // 128) + s
                for kc in range(KC):
                    S.op("pe", lambda e, b=b, s=s, kc=kc, cb=cb: e.matmul(
                        banks[bk][:, cb * 2:cb * 2 + 2], wb[b][:, kc, s * 128:(s + 1) * 128], scT[:, kc, :],
                        start=(kc == 0), stop=(kc == KC - 1)),
                        r=[wb_t[b], scT_t], w=[bankT[bk]])
        S.op("dve", lambda e: e.tensor_tensor(
            modT, banks[bk][:, 0:288].rearrange("p (a b) -> p a b", b=2), bmod, ALU.add),
            r=[bankT[bk], bmod_t], w=[mod_t])
        for i in range(3):
            sh = modT[:, (3 * i) * KC:(3 * i + 1) * KC, :]
            sc = modT[:, (3 * i + 1) * KC:(3 * i + 2) * KC, :]
            g = modT[:, (3 * i + 2) * KC:(3 * i + 3) * KC, :]
            ng = normg[:, i, :]
            S.op("dve", lambda e, i=i, sc=sc, ng=ng: e.scalar_tensor_tensor(
                AB[:, 3 * i, :, :], sc, 1.0, ng.unsqueeze(2).to_broadcast([128, KC, 2]), ALU.add, ALU.mult),
                r=[mod_t, normg_t], w=[ab_t])
            S.op("dve", lambda e, i=i, sh=sh: e.tensor_copy(AB[:, 3 * i + 1, :, :], sh), r=[mod_t], w=[ab_t])
            gs = 1.0 if i == 1 else 0.5
            S.op("dve", lambda e, i=i, g=g, gs=gs: e.tensor_scalar(
                AB[:, 3 * i + 2, :, :], g, gs, None, ALU.mult), r=[mod_t], w=[ab_t])

    def norm_mod(i):
        scratch_reset()
        sq = [new([512], F32) for _ in range(2)]
        sq_t = S.tiles("sq", 2, scr=True)
        rstd = [new([512], F32) for _ in range(2)]
        rstd_t = S.tiles("rstd", 2, scr=True)
        tmp = [new([512], F32) for _ in range(2)]
        tmp_t = S.tiles("ntmp", 2, scr=True)
        n = 0
        for tt in range(NTT):
            cs = tt_cond(tt)
            bk = next_bank()
            for dc in range(KC):
                b = n % 2
                n += 1
                S.op("act", lambda e, b=b, dc=dc, tt=tt: e.activation(
                    sq[b], xT[:, dc, tt * 512:(tt + 1) * 512], AF.Square), r=[xT_t[dc][tt]], w=[sq_t[b]])
                S.op("pe", lambda e, b=b, dc=dc, bk=bk: e.matmul(
                    banks[bk][:, :], ones_mat, sq[b],
                    start=(dc == 0), stop=(dc == KC - 1)), r=[sq_t[b], ones_t], w=[bankT[bk]])
            rb = tt % 2
            S.op("act", lambda e, rb=rb, bk=bk: e.activation(
                rstd[rb], banks[bk][:, :], AF.Sqrt, bias=eps_c, scale=1.0), r=[bankT[bk], eps_t], w=[rstd_t[rb]])
            S.op("dve", lambda e, rb=rb: e.reciprocal(rstd[rb], rstd[rb]), r=[rstd_t[rb]], w=[rstd_t[rb]])
            for dc in range(KC):
                b = n % 2
                n += 1
                S.op("dve", lambda e, b=b, dc=dc, tt=tt, rb=rb, cs=cs: e.scalar_tensor_tensor(
                    tmp[b], xT[:, dc, tt * 512:(tt + 1) * 512], AB[:, 3 * i, dc, cs:cs + 1], rstd[rb],
                    ALU.mult, ALU.mult), r=[xT_t[dc][tt], rstd_t[rb], ab_t], w=[tmp_t[b]])
                S.op("act", lambda e, b=b, dc=dc, tt=tt, cs=cs: e.activation(
                    hT[:, dc, tt * 512:(tt + 1) * 512], tmp[b], AF.Identity,
                    bias=AB[:, 3 * i + 1, dc, cs:cs + 1], scale=1.0), r=[tmp_t[b], ab_t], w=[hT_t[dc][tt]])

    ones_mat = new([128], F32)
    eps_c = new([1], F32)
    S.op("dve", lambda e: e.memset(eps_c, 1e-6), w=[eps_t])
    S.op("dve", lambda e: e.memset(ones_mat, 1.0 / D), w=[ones_t])
    scratch0 = apos[0]
    SCR = ARENA_BYTES - scratch0

    def ffn(l, f, gi):
        scratch_reset()
        G = cfg.G
        act = new([G, NT], BF16)
        act_t = S.tiles("act", G, NTT, scr=True)
        wd = new([G, D], BF16)
        wd_t = S.tiles("wd", G, scr=True)
        wdk = [S.gkey("wd%d" % c) for c in range(G)]
        wg = [new([KC, 256], BF16) for _ in range(2)]
        wg_t = S.tiles("wg", 2, scr=True)
        wgk = [S.gkey("wg%d" % b) for b in range(2)]
        sg = [new([512], F32) for _ in range(2)]
        sg_t = S.tiles("sg", 2, scr=True)
        F = cfg.F
        nsg = 0
        ups = 0
        for g0 in range(0, FC, G):
            chunks = list(range(g0, min(g0 + G, FC)))
            for ci, c in enumerate(chunks):
                b = c % 2
                srcg = ffn_gu[l, f, :, c * 128:(c + 1) * 128].rearrange("(kc p) n -> p kc n", p=128)
                srcu = ffn_gu[l, f, :, F + c * 128:F + (c + 1) * 128].rearrange("(kc p) n -> p kc n", p=128)
                S.op("pool", lambda e, b=b, srcg=srcg: e.dma_start(out=wg[b][:, :, 0:128], in_=srcg),
                     w=[wg_t[b]], key=wgk[b])
                S.op("pool", lambda e, b=b, srcu=srcu: e.dma_start(out=wg[b][:, :, 128:256], in_=srcu),
                     w=[wg_t[b]], key=wgk[b])
                S.op("pool", lambda e, ci=ci, c=c: e.dma_start(out=wd[:, ci, :], in_=ffn_down[l, f, c * 128:(c + 1) * 128, :]),
                     w=[wd_t[ci]], key=wdk[ci])
                for tt in range(NTT):
                    bg = (ups % 2) * 2
                    bu = bg + 1
                    ups += 1
                    for kc in range(KC):
                        S.op("pe", lambda e, b=b, kc=kc, tt=tt, bg=bg: e.matmul(
                            banks[bg][:, :], wg[b][:, kc, 0:128], hT[:, kc, tt * 512:(tt + 1) * 512],
                            start=(kc == 0), stop=(kc == KC - 1)), r=[wg_t[b], hT_t[kc][tt]], w=[bankT[bg]])
                    for kc in range(KC):
                        S.op("pe", lambda e, b=b, kc=kc, tt=tt, bu=bu: e.matmul(
                            banks[bu][:, :], wg[b][:, kc, 128:256], hT[:, kc, tt * 512:(tt + 1) * 512],
                            start=(kc == 0), stop=(kc == KC - 1)), r=[wg_t[b], hT_t[kc][tt]], w=[bankT[bu]])
                    sb = nsg % 2
                    nsg += 1
                    S.op("act", lambda e, sb=sb, bg=bg: e.activation(sg[sb], banks[bg][:, :], AF.Silu),
                         r=[bankT[bg]], w=[sg_t[sb]])
                    S.op("dve", lambda e, sb=sb, bu=bu, ci=ci, tt=tt: e.tensor_tensor(
                        act[:, ci, tt * 512:(tt + 1) * 512], sg[sb], banks[bu][:, :], ALU.mult),
                        r=[sg_t[sb], bankT[bu]], w=[act_t[ci][tt]])
            for j in range(KC):
                for tt in range(NTT):
                    cs = tt_cond(tt)
                    bk = 4 + (next_bank() % 4)
                    for ci, c in enumerate(chunks):
                        S.op("pe", lambda e, bk=bk, ci=ci, j=j, tt=tt: e.matmul(
                            banks[bk][:, :], wd[:, ci, j * 128:(j + 1) * 128], act[:, ci, tt * 512:(tt + 1) * 512],
                            start=(ci == 0), stop=(ci == len(chunks) - 1)),
                            r=[wd_t[ci], act_t[ci][tt]], w=[bankT[bk]])
                    S.op("dve", lambda e, bk=bk, j=j, tt=tt, cs=cs: e.scalar_tensor_tensor(
                        xT[:, j, tt * 512:(tt + 1) * 512], banks[bk][:, :], AB[:, gi, j, cs:cs + 1],
                        xT[:, j, tt * 512:(tt + 1) * 512], ALU.mult, ALU.add),
                        r=[bankT[bk], ab_t, xT_t[j][tt]], w=[xT_t[j][tt]])

    out_keys = []
    dbg_outs = []

    def dump(name, ap, tiles):
        if not getattr(cfg, "dbg", False):
            return
        shp = list(ap.shape)
        flat = ap
        o = nc.dram_tensor("dbg_" + name, shp, ap.dtype, kind="ExternalOutput").ap()
        k = S.key("dbg_" + name)
        out_keys.append(k)
        dbg_outs.append("dbg_" + name)
        S.op("sp", lambda e: e.dma_start(out=o, in_=ap), r=tiles, key=k)

    def alltiles(tt):
        return [t for row in tt for t in row]

    gkey = S.gkey
    if MX:
        svec_t, convw_t, cvec_t, onesbf_t = [S.tiles(n) for n in ("svec", "convw", "cvec", "onesbf")]
        cm_t = ident_t
        S.op("sp", lambda e: e.dma_start(out=cmat.rearrange("p a b -> p (a b)"), in_=cmat_d), w=[cm_t], key=S.gkey("const4"))
        S.op("dve", lambda e: e.memset(cvec[:, 0:1], 1e-6), w=[cvec_t])
        S.op("dve", lambda e: e.memset(cvec[:, 1:2], 1.0), w=[cvec_t])
        S.op("dve", lambda e: e.memset(cvec[:, 2:4], 0.0), w=[cvec_t])
        S.op("dve", lambda e: e.memset(cvec[0:64, 2:3], 1.0), w=[cvec_t])
        S.op("dve", lambda e: e.memset(cvec[64:128, 3:4], 1.0), w=[cvec_t])
        S.op("dve", lambda e: e.memset(ones_bf, 1.0), w=[onesbf_t])
        cmb_t = S.tiles("cmb")
        S.op("act", lambda e: e.activation(cmat_bf, cmat, AF.Copy), r=[cm_t], w=[cmb_t])

        def CMB(name):
            return cmat_bf[:, CMI[name], :]
        EPS = cvec[:, 0:1]
        ONE = cvec[:, 1:2]

        def CM(name):
            return cmat[:, CMI[name], :]

    def mm(out, lhsT, rhs, r, w, start=True, stop=True):
        S.op("pe", lambda e: e.matmul(out, lhsT, rhs, start=start, stop=stop), r=r, w=w)

    def tr(out, in_, r, w):
        S.op("pe", lambda e: e.transpose(out, in_, cmat[0:in_.shape[0], CMI["ident"], 0:in_.shape[0]]), r=list(r) + [cm_t], w=w)

    def act(out, in_, func, r, w, bias=None, scale=1.0):
        if bias is None:
            S.op("act", lambda e: e.activation(out, in_, func, scale=scale), r=r, w=w)
        else:
            S.op("act", lambda e: e.activation(out, in_, func, bias=bias, scale=scale), r=r, w=w)

    def vtt(out, a, b, op, r, w):
        S.op("dve", lambda e: e.tensor_tensor(out, a, b, op), r=r, w=w)

    def vstt(out, in0, scalar, in1, op0, op1, r, w):
        S.op("dve", lambda e: e.scalar_tensor_tensor(out, in0, scalar, in1, op0, op1), r=r, w=w)

    def vts(out, in0, s1, s2, op0, op1, r, w):
        if s2 is None:
            S.op("dve", lambda e: e.tensor_scalar(out, in0, s1, None, op0), r=r, w=w)
        else:
            S.op("dve", lambda e: e.tensor_scalar(out, in0, s1, s2, op0, op1), r=r, w=w)

    def vcp(out, in_, r, w):
        S.op("dve", lambda e: e.tensor_copy(out, in_), r=r, w=w)

    def vrecip(out, in_, r, w):
        S.op("dve", lambda e: e.reciprocal(out, in_), r=r, w=w)

    cp_rr = [0]

    def anycp(out, in_, r, w):
        cp_rr[0] += 1
        if cp_rr[0] % 2:
            vcp(out, in_, r, w)
        else:
            act(out, in_, AF.Copy, r, w)

    def BK(exclude=()):
        while True:
            b = next_bank()
            if bankT[b] not in exclude:
                return banks[b], bankT[b]

    def stile(name, *dims):
        return S.tiles(name, *dims, scr=True)

    TB = [(0, 512, 0), (512, 512, 1), (1024, 512, 1)]
    SEQS = [(0, 256), (256, 256), (512, 1024)]

    def mixers(l):
        scratch_reset()
        uid = [0]

        def key(nm):
            return gkey(nm)

        def sv(col):
            return svec[:, col:col + 1]

        def wpiece(buf, buf_t, cols, src=None):
            k = gkey("wp_" + buf_t.name)
            off = 0
            for (c0, n) in cols:
                sl = w_in[l, :, c0:c0 + n].rearrange("(kc p) n -> p kc n", p=128)
                S.op("pool", lambda e, sl=sl, off=off, n=n: e.dma_start(out=buf[:, :, off:off + n], in_=sl),
                     w=[buf_t], key=k)
                off += n

        def proj_feat(buf, buf_t, c0, M, tb, out_bank):
            t0, n, _ = TB[tb]
            bk, bt = out_bank
            for kc in range(KC):
                mm(bk[0:M, 0:n], buf[:, kc, c0:c0 + M], hT[:, kc, t0:t0 + n], [buf_t, hT_t[kc][tb]], [bt],
                   start=(kc == 0), stop=(kc == KC - 1))

        def proj_tok(buf, buf_t, c0, N, tok0, out_ap, bt):
            tbs = sorted({tok0 // 512, (tok0 + 127) // 512})
            for kc in range(KC):
                mm(out_ap, hT[:, kc, tok0:tok0 + 128], buf[:, kc, c0:c0 + N], [buf_t] + [hT_t[kc][tb] for tb in tbs], [bt],
                   start=(kc == 0), stop=(kc == KC - 1))

        ftmp = [new([512], F32)]
        ftmp_t = stile("ftmp", 1)
        frs = [new([512], F32)]
        frs_t = stile("frs", 1)

        def fnorm(src, src_t, n, out, out_t, gain, mean=True, P=128):
            i = 0
            fb = ftmp[i].bitcast(BF16)
            act(fb[0:P, 0:n], src, AF.Square, [src_t], [ftmp_t[i]])
            bk, bt = BK()
            mm(bk[0:P, 0:n], ones_bf[0:P, 0:P], fb[0:P, 0:n], [ftmp_t[i], onesbf_t], [bt])
            act(frs[i][0:P, 0:n], bk[0:P, 0:n], AF.Sqrt, [bt, cvec_t], [frs_t[i]], bias=EPS[0:P, :],
                scale=(1.0 / 128 if mean else 1.0))
            vrecip(frs[i][0:P, 0:n], frs[i][0:P, 0:n], [frs_t[i]], [frs_t[i]])
            vstt(out, src, gain, frs[i][0:P, 0:n], ALU.mult, ALU.mult, [src_t, frs_t[i], svec_t], [out_t])

        wo = new([D], BF16)
        wo_t = stile("wo")

        def out_proj(row0, mg, mg_t):
            k = gkey("wo")
            S.op("pool", lambda e: e.dma_start(out=wo, in_=w_out[l, row0:row0 + 128, :]), w=[wo_t], key=k)
            for j in range(KC):
                for tb, (t0, n, cs) in enumerate(TB):
                    bk, bt = BK()
                    mm(bk[:, 0:n], wo[:, j * 128:(j + 1) * 128], mg[:, t0:t0 + n], [wo_t, mg_t], [bt])
                    vstt(xT[:, j, t0:t0 + n], bk[:, 0:n], AB[:, 5, j, cs:cs + 1], xT[:, j, t0:t0 + n],
                         ALU.mult, ALU.add, [bt, ab_t, xT_t[j][tb]], [xT_t[j][tb]])

        pt_rr = [0]
        ATT_SCALE = 128.0 ** -0.5

        def attention(qT_ap, q_t, n, ktiles, out_ap, out_t, pt, pt_t, rden, rden_t):
            bo, bot = BK()
            bd, bdt = BK()
            nk = len(ktiles)
            for i, (kT_ap, v_ap, tl, eb) in enumerate(ktiles):
                bs, bst = BK(exclude=(bot, bdt))
                mm(bs[:, 0:n], kT_ap, qT_ap, list(tl) + [q_t], [bst])
                pi = pt_rr[0] % 2
                pt_rr[0] += 1
                act(pt[pi][:, 0:n], bs[:, 0:n], AF.Exp, [bst], [pt_t[pi]], scale=ATT_SCALE)
                if eb is not None:
                    eb_ap, eb_t = eb
                    vtt(pt[pi][:, 0:n], pt[pi][:, 0:n], eb_ap, ALU.mult, [pt_t[pi], eb_t], [pt_t[pi]])
                mm(bo[:, 0:n], v_ap, pt[pi][:, 0:n], list(tl) + [pt_t[pi]], [bot], start=(i == 0), stop=(i == nk - 1))
                mm(bd[:, 0:n], ones_bf, pt[pi][:, 0:n], [onesbf_t, pt_t[pi]], [bdt], start=(i == 0), stop=(i == nk - 1))
            vrecip(rden[:, 0:n], bd[:, 0:n], [bdt], [rden_t])
            vtt(out_ap, bo[:, 0:n], rden[:, 0:n], ALU.mult, [bot, rden_t], [out_t])

        stg = [new([128], F32) for _ in range(2)]
        stg_t = stile("stg", 2)
        stg_k = [gkey("stg0"), gkey("stg1")]
        stg_rr = [0]

        def store_tok(psum_ap, bt, dst):
            i = stg_rr[0] % 2
            stg_rr[0] += 1
            anycp(stg[i], psum_ap, [bt], [stg_t[i]])
            S.op("sp", lambda e: e.dma_start(out=dst, in_=stg[i]), r=[stg_t[i]], key=stg_k[i])

        mg = [new([NT], BF16)]
        mg_t = stile("mg", 1)
        mg_rr = [0]
        base_mark = apos[0]

        def mgsel():
            return 0

        def attn_mixer(kind):
            apos[0] = base_mark
            if kind == "A":
                qc0, kc0, vc0, nq, nkv, gq, gk, row0 = 0, 512, 1024, 4, 4, sv(0), sv(1), 0
                ck_d, cv_d, ok_d, ov_d = c_nak, c_nav, o_nak, o_nav
            else:
                qc0, kc0, vc0, nq, nkv, gq, gk, row0 = 3104, 3616, 3872, 4, 2, sv(2), sv(3), 1024
                ck_d, cv_d, ok_d, ov_d = c_gqk, c_gqv, o_gqk, o_gqv
            qpk = nq // nkv
            nf = [new([512], F32)]
            nf_t = stile("nf", 1)
            wpA = new([KC, 256], BF16)
            wpA_t = stile("wpA" + kind)
            wpB = new([KC, 128], BF16)
            wpB_t = stile("wpB" + kind)
            pt = [new([512], BF16) for _ in range(2)]
            pt_t = stile("pt", 2)
            rden = new([512], F32)
            rden_t = stile("rden")
            kT = new([NT], BF16)
            kT_t = stile("kT")
            nvt = 19 if kind == "A" else 12
            vtok = new([nvt, 128], BF16)
            vtok_t = stile("vtok")
            ckT = new([256], BF16)
            ckT_t = stile("ckT")
            cv = new([2, 128], BF16)
            cv_t = stile("cv")
            ckl = new([2, 128], F32)
            ckl_t = stile("ckl")
            qT = new([NT], BF16)
            qT_t = stile("qT")
            nf_rr = [0]
            if kind == "C":
                rope = new([2, 512], F32)
                rope_t = stile("rope")
                kr = gkey("rope")
                rt1 = new([512], F32)
                rt1_t = stile("rt1")
            else:
                eb = new([14, 64], F32)
                eb_t = stile("eb")

            def normed(buf, buf_t, c0, tb, gain, dst_bf, dst_t, do_rope, want_f32):
                t0, n, _ = TB[tb]
                bk = BK()
                proj_feat(buf, buf_t, c0, 128, tb, bk)
                i = 0
                fnorm(bk[0][:, 0:n], bk[1], n, nf[i][:, 0:n], nf_t[i], gain)
                if do_rope and not getattr(cfg, "norope", False):
                    s0 = t0 - 512
                    for cs_ in range(2):
                        S.op("sp", lambda e, cs_=cs_, s0=s0: e.dma_start(
                            out=rope[:, cs_, :], in_=rope_d[:, cs_ * 1024 + s0:cs_ * 1024 + s0 + 512]), w=[rope_t], key=kr)
                    kx = gkey("ropex")
                    for (a, b) in ((0, 32), (32, 0), (64, 96), (96, 64)):
                        S.op("sp", lambda e, a=a, b=b, n=n: e.dma_start(out=rt1[a:a + 32, 0:n], in_=nf[0][b:b + 32, 0:n]),
                             r=[nf_t[0]], w=[rt1_t], key=kx)
                    vtt(rt1[:, 0:n], rt1[:, 0:n], rope[:, 1, 0:n], ALU.mult, [rt1_t, rope_t], [rt1_t])
                    vtt(nf[i][:, 0:n], nf[i][:, 0:n], rope[:, 0, 0:n], ALU.mult, [nf_t[i], rope_t], [nf_t[i]])
                    vtt(dst_bf[:, t0:t0 + n], nf[i][:, 0:n], rt1[:, 0:n], ALU.add, [nf_t[i], rt1_t], [dst_t])
                else:
                    anycp(dst_bf[:, t0:t0 + n], nf[i][:, 0:n], [nf_t[i]], [dst_t])
                return nf[i], nf_t[i]

            for g in range(nkv):
                wpiece(wpA, wpA_t, [(kc0 + g * 128, 128), (vc0 + g * 128, 128)])
                for tb in range(3):
                    f, f_t = normed(wpA, wpA_t, 0, tb, gk, kT, kT_t, kind == "C" and tb > 0, tb == 0)
                    if tb == 0:
                        for s4 in range(4):
                            bk, bt = BK()
                            tr(bk[:, 0:128], f[:, s4 * 128:(s4 + 1) * 128], [f_t], [bt])
                            store_tok(bk[:, 0:128], bt, ok_d[s4 // 2, l, g, (s4 % 2) * 128:(s4 % 2) * 128 + 128, :])
                vt_list = [ti * 128 for ti in range(12)]
                if kind == "A":
                    vt_list += [512 + 64 + 128 * i for i in range(7)]
                for vi, tok0 in enumerate(vt_list):
                    bk, bt = BK()
                    proj_tok(wpA, wpA_t, 128, 128, tok0, bk[:, 0:128], bt)
                    anycp(vtok[:, vi, :], bk[:, 0:128], [bt], [vtok_t])
                    if vi < 4:
                        store_tok(bk[:, 0:128], bt, ov_d[vi // 2, l, g, (vi % 2) * 128:(vi % 2) * 128 + 128, :])
                kk = gkey("ck")
                S.op("sp", lambda e, g=g: e.dma_start(out=ckl, in_=ck_d[l, g].rearrange("(a p) d -> p a d", p=128)),
                     w=[ckl_t], key=kk)
                S.op("pool", lambda e, g=g: e.dma_start(out=cv, in_=cv_d[l, g].rearrange("(a p) d -> p a d", p=128)),
                     w=[cv_t], key=gkey("cvk"))
                for a in range(2):
                    bk, bt = BK()
                    tr(bk[:, 0:128], ckl[:, a, :], [ckl_t], [bt])
                    anycp(ckT[:, a * 128:(a + 1) * 128], bk[:, 0:128], [bt], [ckT_t])
                for hq in range(qpk):
                    h = g * qpk + hq
                    wpiece(wpB, wpB_t, [(qc0 + h * 128, 128)])
                    for tb in range(3):
                        normed(wpB, wpB_t, 0, tb, gq, qT, qT_t, kind == "C" and tb > 0, False)
                    mi = 0
                    for s_ in range(2):
                        kt = [(kT[:, s_ * 256 + a * 128:s_ * 256 + (a + 1) * 128], vtok[:, s_ * 2 + a, :], [kT_t, vtok_t], None)
                              for a in range(2)]
                        attention(qT[:, s_ * 256:(s_ + 1) * 256], qT_t, 256, kt, mg[mi][:, s_ * 256:(s_ + 1) * 256], mg_t[mi],
                                  pt, pt_t, rden, rden_t)
                    ctx_kt = [(ckT[:, a * 128:(a + 1) * 128], cv[:, a, :], [ckT_t, cv_t], None) for a in range(2)]
                    if getattr(cfg, "nolat", False):
                        pass
                    elif kind == "C":
                        loc = [(kT[:, 512 + a * 128:512 + (a + 1) * 128], vtok[:, 4 + a, :], [kT_t, vtok_t], None) for a in range(8)]
                        for qb in range(2):
                            attention(qT[:, 512 + qb * 512:512 + (qb + 1) * 512], qT_t, 512, ctx_kt + loc,
                                      mg[mi][:, 512 + qb * 512:512 + (qb + 1) * 512], mg_t[mi], pt, pt_t, rden, rden_t)
                    else:
                        ke = gkey("eb")
                        src = rpbx_d[:, ((l * 4 + h) * 14) * 64:((l * 4 + h) * 14 + 14) * 64]
                        S.op("sp", lambda e, src=src: e.dma_start(out=eb.rearrange("p a b -> p (a b)"), in_=src), w=[eb_t], key=ke)
                        act(eb, eb, AF.Exp, [eb_t], [eb_t])
                        for r_ in range(16):
                            k0 = min(max(r_ - 4, 0), 8)
                            loc = []
                            for j in range(4):
                                start = (k0 + 2 * j) * 64
                                vi = 4 + start // 128 if start % 128 == 0 else 12 + (start - 64) // 128
                                dr0 = k0 + 2 * j - r_
                                loc.append((kT[:, 512 + start:512 + start + 128], vtok[:, vi, :], [kT_t, vtok_t],
                                            (eb[:, dr0 + 7, :], eb_t)))
                            attention(qT[:, 512 + r_ * 64:512 + (r_ + 1) * 64], qT_t, 64, ctx_kt + loc,
                                      mg[mi][:, 512 + r_ * 64:512 + (r_ + 1) * 64], mg_t[mi], pt, pt_t, rden, rden_t)
                    out_proj(row0 + h * 128, mg[mi], mg_t[mi])

        if "A" in cfg.mix:
            attn_mixer("A")
        dump("mgA_%d" % l, mg[0], [mg_t[0]])
        if "C" in cfg.mix:
            attn_mixer("C")
        dump("mgC_%d" % l, mg[0], [mg_t[0]])


        def run_chains(d, h, dk, Sf, Sf_t, Sb, Sb_t, state_in, state_out, ks, pre, step_fn):
            jit = [0]
            if getattr(cfg, "bstage", 2) < 0:
                return
            for si, (st0, sn) in enumerate(SEQS):
                ci = si % 2
                if si < 2:
                    S.op("dve", lambda e, ci=ci: e.memset(Sf[0:dk, ci, :], 0.0), w=[Sf_t[ci]])
                else:
                    S.op("sp", lambda e, ci=ci: e.dma_start(out=Sf[0:dk, ci, :], in_=state_in[l, d, h]), w=[Sf_t[ci]], key=ks)
                vcp(Sb[0:dk, ci, :], Sf[0:dk, ci, :], [Sf_t[ci]], [Sb_t[ci]])
                tiles = list(range(st0 // 128, (st0 + sn) // 128))
                if d == 1:
                    tiles = tiles[::-1]
                for ti in tiles:
                    i = jit[0] % 2
                    jit[0] += 1
                    if getattr(cfg, "bstage", 2) >= 1:
                        pre(ti, i)
                    if getattr(cfg, "bstage", 2) >= 2:
                        for cc in ((0, 1) if d == 0 else (1, 0)):
                            step_fn(ti, cc, i, ci)
                if si < 2:
                    S.op("sp", lambda e, ci=ci, si=si: e.dma_start(out=state_out[si, l, d, h], in_=Sf[0:dk, ci, :]),
                         r=[Sf_t[ci]], key=ks)

        def gla_mixer():
            apos[0] = base_mark
            wpA = new([KC, 128], BF16)
            wpA_t = stile("wpAB")
            wpB = new([KC, 256], BF16)
            wpB_t = stile("wpBB")
            gfa = new([NT], BF16)
            gfa_t = stile("gfa")
            gup = new([512], BF16)
            gup_t = stile("gup")
            kg = gkey("gup")
            S.op("pool", lambda e: e.dma_start(out=gup[0:64, :], in_=gup_d[:, l * 512:(l + 1) * 512]), w=[gup_t], key=kg)
            wpiece(wpA, wpA_t, [(3072, 32)])
            S.op("dve", lambda e: e.memset(gfa[0:64, :], 1.0), w=[gfa_t])
            for tb in range(3):
                t0, n, _ = TB[tb]
                bk = BK()
                proj_feat(wpA, wpA_t, 0, 32, tb, bk)
                vcp(gfa[0:32, t0:t0 + n], bk[0][0:32, 0:n], [bk[1]], [gfa_t])
            if getattr(cfg, "bpart", 9) <= 1:
                return
            qTh = new([NT], BF16)
            qTh_t = stile("gqT")
            kTh = new([NT], BF16)
            kTh_t = stile("gkT")
            ktok = new([12, 64], BF16)
            ktok_t = stile("gktok")
            vtok = new([12, 128], BF16)
            vtok_t = stile("gvtok")
            osum = new([NT], F32)
            osum_t = stile("gosum")
            qt_ = new([2, 128], BF16)
            qt_t = stile("gqt", 2)
            kt_ = new([2, 128], BF16)
            kt_t = stile("gkt", 2)
            kdec = new([4, 64], BF16)
            kdec_t = stile("gkdec", 2)
            attT = new([2, 128], BF16)
            attT_t = stile("gatt", 2)
            elast = new([2, 2], F32)
            elast_t = stile("gel", 2)
            la = new([2, 64], F32)
            la_t = stile("gla", 2)
            lah = new([2, 64], BF16)
            lah_t = stile("glah", 2)
            lal = new([2, 64], BF16)
            lal_t = stile("glal", 2)
            t1 = new([2, 64], F32)
            t1_t = stile("gt1", 2)
            t2 = new([1, 128], F32)
            t2_t = stile("gt2", 1)
            Sf = new([2, 128], F32)
            Sf_t = stile("gSf", 2)
            Sb = new([2, 128], BF16)
            Sb_t = stile("gSb", 2)
            ks = gkey("gst")
            for h in range(4):
                wpiece(wpA, wpA_t, [(1536 + h * 64, 64), (1792 + h * 64, 64)])
                wpiece(wpB, wpB_t, [(2048 + h * 128, 128), (2560 + h * 128, 128)])
                for tb in range(3):
                    t0, n, _ = TB[tb]
                    bk = BK()
                    proj_feat(wpA, wpA_t, 0, 64, tb, bk)
                    act(qTh[0:64, t0:t0 + n], bk[0][0:64, 0:n], AF.Identity, [bk[1]], [qTh_t], scale=0.125)
                    bk = BK()
                    proj_feat(wpA, wpA_t, 64, 64, tb, bk)
                    vcp(kTh[0:64, t0:t0 + n], bk[0][0:64, 0:n], [bk[1]], [kTh_t])
                for ti in range(12):
                    bk, bt = BK()
                    proj_tok(wpA, wpA_t, 64, 64, ti * 128, bk[:, 0:64], bt)
                    proj_tok(wpB, wpB_t, 0, 128, ti * 128, bk[:, 128:256], bt)
                    vcp(ktok[:, ti, :], bk[:, 0:64], [bt], [ktok_t])
                    act(vtok[:, ti, :], bk[:, 128:256], AF.Copy, [bt], [vtok_t])
                for d in range(2):
                    cum = CMB("Ublk") if d == 0 else CMB("Lblk")
                    msk = CM("M_ge") if d == 0 else CM("M_le")
                    pb = d * 32

                    def pre(ti, i, d=d, cum=cum, msk=msk, pb=pb, h=h):
                        tok = slice(ti * 128, (ti + 1) * 128)
                        bk, bt = BK()
                        mm(bk[:, 0:64], gfa[0:64, tok], gup[0:64, d * 256 + h * 64:d * 256 + (h + 1) * 64], [gfa_t, gup_t], [bt])
                        act(la[:, i, :], bk[:, 0:64], AF.Exp, [bt], [la_t[i]], scale=-1.0)
                        act(la[:, i, :], la[:, i, :], AF.Ln, [la_t[i], cvec_t], [la_t[i]], bias=ONE)
                        vts(la[:, i, :], la[:, i, :], -1.0 / 16.0, None, ALU.mult, None, [la_t[i]], [la_t[i]])
                        act(lah[:, i, :], la[:, i, :], AF.Copy, [la_t[i]], [lah_t[i]])
                        vtt(lal[:, i, :], la[:, i, :], lah[:, i, :], ALU.subtract, [la_t[i], lah_t[i]], [lal_t[i]])
                        bk, bt = BK()
                        mm(bk[:, 0:64], cum, lah[:, i, :], [cmb_t, lah_t[i]], [bt], start=True, stop=False)
                        mm(bk[:, 0:64], cum, lal[:, i, :], [cmb_t, lal_t[i]], [bt], start=False, stop=True)
                        mm(bk[:, 64:128], CMB("Bones"), lah[:, i, :], [cmb_t, lah_t[i]], [bt], start=True, stop=False)
                        mm(bk[:, 64:128], CMB("Bones"), lal[:, i, :], [cmb_t, lal_t[i]], [bt], start=False, stop=True)
                        act(t1[:, i, :], bk[:, 0:64], AF.Copy, [bt], [t1_t[i]])
                        vtt(t1[:, i, :], bk[:, 64:128], t1[:, i, :], ALU.subtract, [bt, t1_t[i]], [t1_t[i]])
                        act(t1[:, i, :], t1[:, i, :], AF.Exp, [t1_t[i]], [t1_t[i]])
                        vtt(t1[:, i, :], ktok[:, ti, :], t1[:, i, :], ALU.mult, [ktok_t, t1_t[i]], [t1_t[i]])
                        for cc_ in range(2):
                            vts(kdec[:, i * 2 + cc_, :], t1[:, i, :], cvec[:, 2 + cc_:3 + cc_], None, ALU.mult, None,
                                [t1_t[i], cvec_t], [kdec_t[i]])
                        bk, bt = BK()
                        mm(bk[0:64, 0:128], lah[:, i, :], cum, [cmb_t, lah_t[i]], [bt], start=True, stop=False)
                        mm(bk[0:64, 0:128], lal[:, i, :], cum, [cmb_t, lal_t[i]], [bt], start=False, stop=True)
                        act(t2[0:64, 0, :], bk[0:64, 0:128], AF.Exp, [bt], [t2_t[0]])
                        vtt(qt_[0:64, i, :], qTh[0:64, tok], t2[0:64, 0, :], ALU.mult, [qTh_t, t2_t[0]], [qt_t[i]])
                        lc = 63 if d == 0 else 0
                        vcp(elast[0:64, i, :], t2[0:64, 0, :].rearrange("p (c t) -> p c t", c=2)[:, :, lc], [t2_t[0]], [elast_t[i]])
                        act(t2[0:64, 0, :], bk[0:64, 0:128], AF.Exp, [bt, t2_t[0]], [t2_t[0]], scale=-1.0)
                        vtt(kt_[0:64, i, :], kTh[0:64, tok], t2[0:64, 0, :], ALU.mult, [kTh_t, t2_t[0]], [kt_t[i]])
                        bk, bt = BK()
                        mm(bk[:, 0:128], kt_[0:64, i, :], qt_[0:64, i, :], [kt_t[i], qt_t[i]], [bt])
                        vtt(attT[:, i, :], bk[:, 0:128], msk, ALU.mult, [bt, cm_t], [attT_t[i]])

                    def step_fn(ti, cc, i, ci, d=d):
                        pr = slice(cc * 64, cc * 64 + 64)
                        t0 = ti * 128 + cc * 64
                        bo, bot = BK()
                        mm(bo[:, 0:64], Sb[0:64, ci, :], qt_[0:64, i, cc * 64:cc * 64 + 64], [Sb_t[ci], qt_t[i]], [bot],
                           start=True, stop=False)
                        mm(bo[:, 0:64], vtok[:, ti, :], attT[:, i, cc * 64:cc * 64 + 64], [vtok_t, attT_t[i]], [bot],
                           start=False, stop=True)
                        if d == 0:
                            vcp(osum[:, t0:t0 + 64], bo[:, 0:64], [bot], [osum_t])
                        else:
                            vtt(osum[:, t0:t0 + 64], bo[:, 0:64], osum[:, t0:t0 + 64], ALU.add, [bot, osum_t], [osum_t])
                        bs, bst = BK()
                        mm(bs[0:64, 0:128], kdec[:, i * 2 + cc, :], vtok[:, ti, :], [kdec_t[i], vtok_t], [bst])
                        vstt(Sf[0:64, ci, :], Sf[0:64, ci, :], elast[0:64, i, cc:cc + 1], bs[0:64, 0:128], ALU.mult, ALU.add,
                             [Sf_t[ci], elast_t[i], bst], [Sf_t[ci]])
                        vcp(Sb[0:64, ci, :], Sf[0:64, ci, :], [Sf_t[ci]], [Sb_t[ci]])

                    run_chains(d, h, 64, Sf, Sf_t, Sb, Sb_t, s_gla, o_gla, ks, pre, step_fn)
                if getattr(cfg, "bpart", 9) <= 2:
                    continue
                for tb, (t0, n, _) in enumerate(TB):
                    bk = BK()
                    proj_feat(wpB, wpB_t, 128, 128, tb, bk)
                    act(qTh[:, t0:t0 + n], bk[0][:, 0:n], AF.Silu, [bk[1]], [qTh_t])
                    fnorm(osum[:, t0:t0 + n], osum_t, n, osum[:, t0:t0 + n], osum_t, sv(4))
                    vtt(mg[0][:, t0:t0 + n], osum[:, t0:t0 + n], qTh[:, t0:t0 + n], ALU.mult, [osum_t, qTh_t], [mg_t[0]])
                out_proj(512 + h * 128, mg[0], mg_t[0])

        if "B" in cfg.mix:
            gla_mixer()
        dump("mgB_%d" % l, mg[0], [mg_t[0]])

        def gdn_mixer():
            apos[0] = base_mark
            wpA = new([KC, 16], BF16)
            wpA_t = stile("wpAD")
            wpB = new([KC, 128], BF16)
            wpB_t = stile("wpBD")
            beta = new([12, 8], F32)
            beta_t = stile("dbeta")
            gg = new([12, 8], F32)
            gg_t = stile("dg")
            negA = new([8], F32)
            negA_t = stile("dnegA")
            act(negA, svec[:, 14:22], AF.Exp, [svec_t], [negA_t])
            vts(negA, negA, -1.0, None, ALU.mult, None, [negA_t], [negA_t])
            gt = new([2, 8], F32)
            gt_t = stile("dgt", 2)
            gh = new([12, 8], BF16)
            gh_t = stile("dgh")
            gl = new([12, 8], BF16)
            gl_t = stile("dgl")
            ball = new([12, 8], F32)
            ball_t = stile("dball")
            tall = new([12, 8], F32)
            tall_t = stile("dtall")
            wpiece(wpA, wpA_t, [(6176, 16)])
            for ti in range(12):
                i = ti % 2
                bk, bt = BK()
                proj_tok(wpA, wpA_t, 0, 16, ti * 128, bk[:, 0:16], bt)
                act(gt[:, i, :], bk[:, 0:8], AF.Exp, [bt], [gt_t[i]], scale=-1.0)
                vts(gt[:, i, :], gt[:, i, :], 1.0, None, ALU.add, None, [gt_t[i]], [gt_t[i]])
                vrecip(beta[:, ti, :], gt[:, i, :], [gt_t[i]], [beta_t])
                vtt(gt[:, i, :], bk[:, 8:16], svec[:, 6:14], ALU.add, [bt, svec_t, gt_t[i]], [gt_t[i]])
                act(gt[:, i, :], gt[:, i, :], AF.Exp, [gt_t[i]], [gt_t[i]])
                act(gt[:, i, :], gt[:, i, :], AF.Ln, [gt_t[i], cvec_t], [gt_t[i]], bias=ONE)
                vtt(gg[:, ti, :], gt[:, i, :], negA, ALU.mult, [gt_t[i], negA_t], [gg_t])
                act(gh[:, ti, :], gg[:, ti, :], AF.Copy, [gg_t], [gh_t])
                vtt(gl[:, ti, :], gg[:, ti, :], gh[:, ti, :], ALU.subtract, [gg_t, gh_t], [gl_t])
                bk, bt = BK()
                for d in range(2):
                    cb = CMB("Ublk") if d == 0 else CMB("Lblk")
                    cs4 = slice(d * 4, d * 4 + 4)
                    mm(bk[:, d * 4:d * 4 + 4], cb, gh[:, ti, cs4], [cmb_t, gh_t], [bt], start=True, stop=False)
                    mm(bk[:, d * 4:d * 4 + 4], cb, gl[:, ti, cs4], [cmb_t, gl_t], [bt], start=False, stop=True)
                    mm(bk[:, 8 + d * 4:12 + d * 4], CMB("Bones"), gh[:, ti, cs4], [cmb_t, gh_t], [bt], start=True, stop=False)
                    mm(bk[:, 8 + d * 4:12 + d * 4], CMB("Bones"), gl[:, ti, cs4], [cmb_t, gl_t], [bt], start=False, stop=True)
                vcp(ball[:, ti, :], bk[:, 0:8], [bt], [ball_t])
                vcp(tall[:, ti, :], bk[:, 8:16], [bt], [tall_t])
            if getattr(cfg, "bpart", 9) <= 1:
                return
            qh = new([NT], BF16)
            qh_t = stile("dqh")
            kh = new([NT], BF16)
            kh_t = stile("dkh")
            ktok = new([12, 128], BF16)
            ktok_t = stile("dktok")
            vtok = new([12, 128], BF16)
            vtok_t = stile("dvtok")
            nft = new([512], F32)
            nft_t = stile("dnft")
            cols = new([2, 8], F32)
            cols_t = stile("dcols", 2)
            vnew = new([2, 128], BF16)
            vnew_t = stile("dvnew", 2)
            Sf = new([2, 128], F32)
            Sf_t = stile("dSf", 2)
            Sb = new([2, 128], BF16)
            Sb_t = stile("dSb", 2)
            ks = gkey("dst")
            PADN = 1548
            r1_mark = apos[0]
            CB = [(0, 0, 256), (256, 260, 256), (512, 520, 512), (1024, 1032, 512)]

            for h in range(4):
                apos[0] = r1_mark
                xpad = new([PADN], F32)
                xpad_t = stile("dxpad")
                acc = new([PADN], F32)
                acc_t = stile("dacc")
                S.op("dve", lambda e, xpad=xpad: e.memset(xpad, 0.0), w=[xpad_t])

                def conv_chunk(c0, chunk_idx):
                    wpiece(wpB, wpB_t, [(c0, 128)])
                    for tb in range(3):
                        bk = BK()
                        proj_feat(wpB, wpB_t, 0, 128, tb, bk)
                        if tb == 0:
                            vcp(xpad[:, 2:258], bk[0][:, 0:256], [bk[1]], [xpad_t])
                            vcp(xpad[:, 262:518], bk[0][:, 256:512], [bk[1]], [xpad_t])
                        else:
                            o = 522 + (tb - 1) * 512
                            vcp(xpad[:, o:o + 512], bk[0][:, 0:512], [bk[1]], [xpad_t])
                    NO = PADN - 4
                    wrow = convw[:, chunk_idx, :]
                    vts(acc[:, 0:NO], xpad[:, 0:NO], wrow[:, 0:1], None, ALU.mult, None, [xpad_t, convw_t], [acc_t])
                    for j in range(1, 5):
                        vstt(acc[:, 0:NO], xpad[:, j:j + NO], wrow[:, j:j + 1], acc[:, 0:NO], ALU.mult, ALU.add,
                             [xpad_t, convw_t, acc_t], [acc_t])
                    act(acc[:, 0:NO], acc[:, 0:NO], AF.Silu, [acc_t], [acc_t])

                conv_chunk(4128 + h * 128, h)
                for (t0, ao, n) in CB:
                    fnorm(acc[:, ao:ao + n], acc_t, n, qh[:, t0:t0 + n], qh_t, 128.0 ** -0.5, mean=False)
                conv_chunk(4128 + 512 + h * 128, 4 + h)
                for (t0, ao, n) in CB:
                    fnorm(acc[:, ao:ao + n], acc_t, n, nft[:, 0:n], nft_t, 1.0, mean=False)
                    vcp(kh[:, t0:t0 + n], nft[:, 0:n], [nft_t], [kh_t])
                    for s4 in range(n // 128):
                        bk, bt = BK()
                        tr(bk[:, 0:128], nft[:, s4 * 128:(s4 + 1) * 128], [nft_t], [bt])
                        vcp(ktok[:, t0 // 128 + s4, :], bk[:, 0:128], [bt], [ktok_t])
                conv_chunk(4128 + 1024 + h * 128, 8 + h)
                for (t0, ao, n) in CB:
                    for s4 in range(n // 128):
                        bk, bt = BK()
                        tr(bk[:, 0:128], acc[:, ao + s4 * 128:ao + (s4 + 1) * 128], [acc_t], [bt])
                        vcp(vtok[:, t0 // 128 + s4, :], bk[:, 0:128], [bt], [vtok_t])
                apos[0] = r1_mark
                osum = new([NT], F32)
                osum_t = stile("dosum")
                M = {}
                for nm in ("Dkq", "Dqk", "A", "ebb"):
                    M[nm] = (new([128], F32), stile("d" + nm))
                XH = new([6, 128], BF16)
                XH_t = stile("dXH", 6)
                XL = new([6, 128], BF16)
                XL_t = stile("dXL", 6)
                GrH = new([128], BF16)
                GrH_t = stile("dGrH")
                GrL = new([128], BF16)
                GrL_t = stile("dGrL")
                kbe = new([128], BF16)
                kbe_t = stile("dkbe")
                vb = new([128], BF16)
                vb_t = stile("dvb")
                utok = new([2, 128], BF16)
                utok_t = stile("dutok", 2)
                wT = new([2, 128], BF16)
                wT_t = stile("dwT", 2)
                attT = new([2, 128], BF16)
                attT_t = stile("datt", 2)
                qt_ = new([2, 128], BF16)
                qt_t = stile("dqt", 2)
                kdec = new([4, 128], BF16)
                kdec_t = stile("dkdec", 2)
                elast = new([2, 2], F32)
                elast_t = stile("del", 2)
                for d in range(2):
                    cum = CMB("Ublk") if d == 0 else CMB("Lblk")
                    nm_kq = CM("NM_ge") if d == 0 else CM("NM_le")
                    nm_qk = CM("NM_lt") if d == 0 else CM("NM_gt")
                    gc = d * 4 + h

                    def pre(ti, i, d=d, cum=cum, nm_kq=nm_kq, nm_qk=nm_qk, gc=gc, M=M, GrH=GrH, GrH_t=GrH_t, GrL=GrL,
                            GrL_t=GrL_t, XH=XH, XH_t=XH_t, XL=XL, XL_t=XL_t, kbe=kbe,
                            kbe_t=kbe_t, vb=vb, vb_t=vb_t, utok=utok, utok_t=utok_t, wT=wT, wT_t=wT_t, attT=attT,
                            attT_t=attT_t, qt_=qt_, qt_t=qt_t, kdec=kdec, kdec_t=kdec_t, elast=elast, elast_t=elast_t):
                        tok = slice(ti * 128, (ti + 1) * 128)
                        gcol = gg[:, ti, gc:gc + 1]
                        bcol_ = beta[:, ti, gc:gc + 1]
                        co = cols[:, i, :]
                        co_t = cols_t[i]
                        vcp(GrH, gh[:, ti, gc:gc + 1].to_broadcast([128, 128]), [gh_t], [GrH_t])
                        vcp(GrL, gl[:, ti, gc:gc + 1].to_broadcast([128, 128]), [gl_t], [GrL_t])
                        bb, bbt = BK()
                        mm(bb[:, 0:128], GrH, cum, [GrH_t, cmb_t], [bbt], start=True, stop=False)
                        mm(bb[:, 0:128], GrL, cum, [GrL_t, cmb_t], [bbt], start=False, stop=True)
                        vcp(co[:, 0:1], ball[:, ti, gc:gc + 1], [ball_t], [co_t])
                        vcp(co[:, 1:2], tall[:, ti, gc:gc + 1], [tall_t], [co_t])
                        vts(co[:, 2:3], co[:, 0:1], -1.0, None, ALU.mult, None, [co_t], [co_t])
                        vts(co[:, 5:6], bcol_, -1.0, None, ALU.mult, None, [beta_t, co_t], [co_t])
                        act(co[:, 3:4], co[:, 0:1], AF.Exp, [co_t], [co_t])
                        vtt(co[:, 3:4], co[:, 3:4], bcol_, ALU.mult, [co_t, beta_t], [co_t])
                        vtt(co[:, 4:5], co[:, 1:2], co[:, 0:1], ALU.subtract, [co_t], [co_t])
                        act(co[:, 4:5], co[:, 4:5], AF.Exp, [co_t], [co_t])
                        Dkq, Dkq_t = M["Dkq"]
                        Dqk, Dqk_t = M["Dqk"]
                        vtt(Dkq, bb[:, 0:128], nm_kq, ALU.add, [bbt, cm_t], [Dkq_t])
                        act(Dkq, Dkq, AF.Exp, [Dkq_t, co_t], [Dkq_t], bias=co[:, 2:3])
                        vstt(Dqk, bb[:, 0:128], -1.0, nm_qk, ALU.mult, ALU.add, [bbt, cm_t], [Dqk_t])
                        act(Dqk, Dqk, AF.Exp, [Dqk_t, co_t], [Dqk_t], bias=co[:, 0:1])
                        ebb, ebb_t = M["ebb"]
                        act(ebb, bb[:, 0:128], AF.Exp, [bbt], [ebb_t])
                        vtt(qt_[:, i, :], qh[:, tok], ebb, ALU.mult, [qh_t, ebb_t], [qt_t[i]])
                        lc = 63 if d == 0 else 0
                        vcp(elast[:, i, :], ebb.rearrange("p (c t) -> p c t", c=2)[:, :, lc], [ebb_t], [elast_t[i]])
                        A0, A0_t = M["A"]

                        def split(src, src_tiles, k, n=1):
                            dh = XH[:, k, :] if n == 1 else XH[:, k:k + n, :]
                            dl = XL[:, k, :] if n == 1 else XL[:, k:k + n, :]
                            th = [XH_t[k + j] for j in range(n)]
                            tl = [XL_t[k + j] for j in range(n)]
                            vcp(dh, src, src_tiles, th)
                            vtt(dl, src, dh, ALU.subtract, list(src_tiles) + th, tl)

                        def mm3(out, bt_, a, b, first=True, last=True):
                            mm(out, XH[:, a, :], XH[:, b, :], [XH_t[a], XH_t[b]], [bt_], start=first, stop=False)
                            mm(out, XH[:, a, :], XL[:, b, :], [XH_t[a], XL_t[b]], [bt_], start=False, stop=False)
                            mm(out, XL[:, a, :], XH[:, b, :], [XL_t[a], XH_t[b]], [bt_], start=False, stop=last)

                        bk, bt = BK()
                        mm(bk[:, 0:128], kh[:, tok], kh[:, tok], [kh_t], [bt])
                        vstt(A0, bk[:, 0:128], co[:, 5:6], Dqk, ALU.mult, ALU.mult, [bt, co_t, Dqk_t], [A0_t])
                        split(A0, [A0_t], 0)
                        bk, bt = BK()
                        tr(bk[:, 0:128], A0, [A0_t], [bt])
                        split(bk[:, 0:128], [bt], 1)
                        vtt(A0, bk[:, 0:128], CM("ident"), ALU.add, [bt, cm_t, A0_t], [A0_t])
                        split(A0, [A0_t], 4)
                        ca, cq = 0, 0
                        for it in range(1, 6):
                            na = 1 - ca
                            Ac, ATc, An = 2 * ca, 2 * ca + 1, 2 * na
                            bk, bt = BK()
                            mm3(bk[:, 0:128], bt, ATc, Ac)
                            if it < 5:
                                mm3(bk[:, 128:256], bt, Ac, ATc)
                                split(bk[:, 0:256].rearrange("p (a b) -> p a b", a=2), [bt], An, n=2)
                            else:
                                split(bk[:, 0:128], [bt], An)
                            bq, bqt = BK()
                            mm(bq[:, 0:128], CMB("ident"), XH[:, 4 + cq, :], [cmb_t, XH_t[4 + cq]], [bqt], start=True, stop=False)
                            mm(bq[:, 0:128], CMB("ident"), XL[:, 4 + cq, :], [cmb_t, XL_t[4 + cq]], [bqt], start=False, stop=False)
                            mm3(bq[:, 0:128], bqt, An, 4 + cq, first=False, last=True)
                            split(bq[:, 0:128], [bqt], 4 + (1 - cq))
                            ca, cq = na, 1 - cq
                        TTb = XH[:, 4 + cq, :]
                        TTb_t = XH_t[4 + cq]
                        act(kbe, ktok[:, ti, :], AF.Copy, [ktok_t, co_t], [kbe_t], scale=co[:, 3:4])
                        act(vb, vtok[:, ti, :], AF.Copy, [vtok_t, beta_t], [vb_t], scale=bcol_)
                        bk, bt = BK()
                        mm(bk[:, 0:128], TTb, vb, [TTb_t, vb_t], [bt])
                        mm(bk[:, 128:256], kbe, TTb, [TTb_t, kbe_t], [bt])
                        vcp(utok[:, i, :], bk[:, 0:128], [bt], [utok_t[i]])
                        act(wT[:, i, :], bk[:, 128:256], AF.Copy, [bt], [wT_t[i]])
                        bk, bt = BK()
                        mm(bk[:, 0:128], kh[:, tok], qh[:, tok], [kh_t, qh_t], [bt])
                        vtt(attT[:, i, :], bk[:, 0:128], Dkq, ALU.mult, [bt, Dkq_t], [attT_t[i]])
                        for cc_ in range(2):
                            vts(kdec[:, i * 2 + cc_, :], ktok[:, ti, :], co[:, 4:5], cvec[:, 2 + cc_:3 + cc_], ALU.mult, ALU.mult,
                                [ktok_t, co_t, cvec_t], [kdec_t[i]])

                    def step_fn(ti, cc, i, ci, d=d, utok=utok, utok_t=utok_t, wT=wT, wT_t=wT_t, attT=attT, attT_t=attT_t,
                                qt_=qt_, qt_t=qt_t, kdec=kdec, kdec_t=kdec_t, elast=elast, elast_t=elast_t, osum=osum,
                                osum_t=osum_t):
                        pr = slice(cc * 64, cc * 64 + 64)
                        t0 = ti * 128 + cc * 64
                        bw, bwt = BK()
                        mm(bw[:, 0:128], wT[:, i, :], Sb[:, ci, :], [wT_t[i], Sb_t[ci]], [bwt])
                        vtt(vnew[:, cc, :], utok[:, i, :], bw[:, 0:128], ALU.subtract, [utok_t[i], bwt], [vnew_t[cc]])
                        bo, bot = BK()
                        mm(bo[:, 0:64], Sb[:, ci, :], qt_[:, i, cc * 64:cc * 64 + 64], [Sb_t[ci], qt_t[i]], [bot],
                           start=True, stop=False)
                        mm(bo[:, 0:64], vnew[:, cc, :], attT[:, i, cc * 64:cc * 64 + 64], [vnew_t[cc], attT_t[i]], [bot],
                           start=False, stop=True)
                        if d == 0:
                            vcp(osum[:, t0:t0 + 64], bo[:, 0:64], [bot], [osum_t])
                        else:
                            vtt(osum[:, t0:t0 + 64], bo[:, 0:64], osum[:, t0:t0 + 64], ALU.add, [bot, osum_t], [osum_t])
                        bs, bst = BK()
                        mm(bs[:, 0:128], kdec[:, i * 2 + cc, :], vnew[:, cc, :], [kdec_t[i], vnew_t[cc]], [bst])
                        vstt(Sf[:, ci, :], Sf[:, ci, :], elast[:, i, cc:cc + 1], bs[:, 0:128], ALU.mult, ALU.add,
                             [Sf_t[ci], elast_t[i], bst], [Sf_t[ci]])
                        vcp(Sb[:, ci, :], Sf[:, ci, :], [Sf_t[ci]], [Sb_t[ci]])

                    run_chains(d, h, 128, Sf, Sf_t, Sb, Sb_t, s_gdn, o_gdn, ks, pre, step_fn)
                wpiece(wpB, wpB_t, [(5664 + h * 128, 128)])
                for tb, (t0, n, _) in enumerate(TB):
                    bk = BK()
                    proj_feat(wpB, wpB_t, 0, 128, tb, bk)
                    act(qh[:, t0:t0 + n], bk[0][:, 0:n], AF.Silu, [bk[1]], [qh_t])
                    fnorm(osum[:, t0:t0 + n], osum_t, n, osum[:, t0:t0 + n], osum_t, sv(5))
                    vtt(mg[0][:, t0:t0 + n], osum[:, t0:t0 + n], qh[:, t0:t0 + n], ALU.mult, [osum_t, qh_t], [mg_t[0]])
                out_proj(1536 + h * 128, mg[0], mg_t[0])

        if "D" in cfg.mix:
            gdn_mixer()
        dump("mgD_%d" % l, mg[0], [mg_t[0]])


    def store_y():
        scratch_reset()
        yb = [new([D], F32) for _ in range(2)]
        yb_t = S.tiles("yb", 2, 4, scr=True)
        yk = [S.key("yb0"), S.key("yb1")]
        out_keys.extend(yk)
        ev = 0
        for ti in range(NT // 128):
            b = ti % 2
            tt = ti // 4
            for q in range(4):
                bk = next_bank()
                for j in range(4):
                    dc = q * 4 + j
                    S.op("pe", lambda e, bk=bk, j=j, dc=dc, ti=ti: e.transpose(
                        banks[bk][:, j * 128:(j + 1) * 128], xT[:, dc, ti * 128:(ti + 1) * 128], ident),
                        r=[xT_t[dc][tt], ident_t], w=[bankT[bk]])
                eng = "dve" if ev % 2 == 0 else "act"
                ev += 1
                dst = yb[b][:, q * 512:(q + 1) * 512]
                if eng == "dve":
                    S.op("dve", lambda e, dst=dst, bk=bk: e.tensor_copy(dst, banks[bk][:, :]),
                         r=[bankT[bk]], w=[yb_t[b][q]])
                else:
                    S.op("act", lambda e, dst=dst, bk=bk: e.activation(dst, banks[bk][:, :], AF.Copy),
                         r=[bankT[bk]], w=[yb_t[b][q]])
            S.op("sp", lambda e, b=b, ti=ti: e.dma_start(out=y[ti * 128:(ti + 1) * 128, :], in_=yb[b]),
                 r=yb_t[b], key=yk[b])

    load_x()
    dump("xT0", xT, alltiles(xT_t))
    for l in range(L):
        adaln(l)
        if l == 0:
            dump("modT", modT, [mod_t])
            dump("AB", AB, [ab_t])
        norm_mod(0)
        if l == 0:
            dump("hT0", hT, alltiles(hT_t))
        ffn(l, 0, 2)
        if l == 0:
            dump("xT1", xT, alltiles(xT_t))
        if cfg.mixers:
            norm_mod(1)
            mixers(l)
        norm_mod(2)
        ffn(l, 1, 8)
    store_y()
    fin = S.op("sp", lambda e: e.nop(), r=[], w=[])
    for k in list(out_keys) + [S.gkey(n) for n in ("stg0", "stg1", "gst", "dst")]:
        fin.deps[("dma", k)] = k.count
    S.emit()
    st.close()
    return nc


def host_consts():
    f32 = np.float32
    idx = np.arange(128)
    p, f = idx[:, None], idx[None, :]
    same = (p // 64) == (f // 64)
    RT = np.zeros((128, 128), f32)
    for m in range(128):
        if m % 64 < 32:
            RT[m + 32, m] = -1.0
        else:
            RT[m - 32, m] = 1.0
    NEG = -30000.0
    mats = dict(
        ident=np.eye(128), ones128=np.full((128, 128), 1.0 / 128), ones1=np.ones((128, 128)), RT=RT,
        Ublk=(same & (p <= f)), Lblk=(same & (p >= f)), Bones=same,
        NM_le=np.where(same & (f <= p), 0.0, NEG), NM_lt=np.where(same & (f < p), 0.0, NEG),
        NM_ge=np.where(same & (f >= p), 0.0, NEG), NM_gt=np.where(same & (f > p), 0.0, NEG),
        M_ge=(same & (f >= p)), M_le=(same & (f <= p)))
    cm = np.stack([np.asarray(mats[n], f32) for n in CM_NAMES], axis=1)
    cmat = np.ascontiguousarray(cm).reshape(128, NCM * 128)
    t = np.arange(1024)
    row = (t // 64).astype(f32)
    col = (t % 64).astype(f32)
    inv = (np.float32(10000.0) ** (-np.arange(0, 64, 2, dtype=f32) / np.float32(64))).astype(f32)
    d = np.arange(128)
    pos = np.where((d // 64)[:, None] == 0, row[None, :], col[None, :]).astype(f32)
    ang = (pos * inv[d % 32][:, None]).astype(f32)
    sgn = np.where((d % 64) < 32, -1.0, 1.0).astype(f32)[:, None]
    rope = np.concatenate([np.cos(ang), np.sin(ang) * sgn], axis=1).astype(f32)
    return cmat, np.ascontiguousarray(rope)


def make_in_maps(cfg, inp, n_cores):
    L, KC = cfg.depth, cfg.KC
    f32 = np.float32
    normgT = np.ascontiguousarray(
        np.asarray(inp["norm_g"], f32).reshape(L, 3, KC, 128).transpose(3, 0, 1, 2)).reshape(128, L * 3 * KC)
    bm = np.asarray(inp["b_mod"], f32).reshape(L, 144, 128).transpose(2, 0, 1)
    bmodT = np.ascontiguousarray(np.repeat(bm[:, :, :, None], 2, axis=3)).reshape(128, L * 144 * 2)
    ident = np.eye(128, dtype=f32)
    shared = dict(normgT=normgT, bmodT=bmodT, ident=ident,
                  w_mod=np.asarray(inp["w_mod"], f32), ffn_gu=np.asarray(inp["ffn_gu"], f32),
                  ffn_down=np.asarray(inp["ffn_down"], f32))
    if cfg.mixers:
        cmat, rope = host_consts()
        sv = np.zeros((128, L, NSV), f32)
        sv[:, :, 0] = np.asarray(inp["na_qk_norm"], f32)[:, 0, :].T
        sv[:, :, 1] = np.asarray(inp["na_qk_norm"], f32)[:, 1, :].T
        sv[:, :, 2] = np.asarray(inp["gqa_qk_norm"], f32)[:, 0, :].T
        sv[:, :, 3] = np.asarray(inp["gqa_qk_norm"], f32)[:, 1, :].T
        sv[:, :, 4] = np.asarray(inp["gla_out_norm"], f32).T
        sv[:, :, 5] = np.asarray(inp["gdn_out_norm"], f32).T
        sv[:, :, 6:14] = np.asarray(inp["gdn_dt_bias"], f32).reshape(L, 8)[None]
        sv[:, :, 14:22] = np.asarray(inp["gdn_a_log"], f32).reshape(L, 8)[None]
        convw = np.ascontiguousarray(
            np.asarray(inp["gdn_conv"], f32).reshape(L, 5, 12, 128).transpose(3, 0, 2, 1)).reshape(128, L * 12 * 5)
        gup = np.zeros((64, L, 2, 256), f32)
        gu = np.asarray(inp["gla_gate_up"], f32)
        gb = np.asarray(inp["gla_gate_bias"], f32)
        for d in range(2):
            gup[d * 16:d * 16 + 16, :, d, :] = gu[:, d].transpose(1, 0, 2)
            gup[32, :, d, :] = gb[:, d]
        rpb = np.asarray(inp["na_rpb"], f32)
        ck = np.arange(64)[:, None]
        cq = np.arange(64)[None, :]
        c0 = np.clip(cq - 8, 0, 48)
        inwin = (ck >= c0) & (ck < c0 + 16)
        dc = np.clip(ck - cq + 15, 0, 30)
        rpbx = np.full((2, 64, L, 4, 14, 64), -30000.0, f32)
        for rs in range(2):
            for di in range(14):
                g = rpb[:, :, di + rs, :][:, :, dc]
                g = np.where(inwin[None, None], g, np.float32(-30000.0))
                rpbx[rs, :, :, :, di, :] = g.transpose(2, 0, 1, 3)
        shared.update(cmat=cmat, rope=rope, svec=np.ascontiguousarray(sv).reshape(128, L * NSV), convw=convw,
                      gup=np.ascontiguousarray(gup).reshape(64, L * 512),
                      rpbx=np.ascontiguousarray(rpbx).reshape(128, L * 4 * 14 * 64),
                      w_in=np.asarray(inp["w_in"], f32), w_out=np.asarray(inp["w_out"], f32))
    maps = []
    xp = np.asarray(inp["x_prompt"], f32)
    xs = np.asarray(inp["x_sample"], f32)
    c = np.asarray(inp["c"], f32)
    cctx = np.asarray(inp["c_ctx"], f32)
    for i in range(n_cores):
        m = dict(shared)
        m["xin"] = np.ascontiguousarray(np.concatenate([xp[2 * i], xp[2 * i + 1], xs[i]], axis=0))
        cc = np.stack([cctx, c[i]], axis=-1)
        m["condT"] = np.ascontiguousarray(cc.reshape(KC, 128, 2).transpose(1, 0, 2)).reshape(128, KC * 2)
        if cfg.mixers:
            m["c_nak"] = np.ascontiguousarray(np.asarray(inp["cache_na_k"], f32)[i])
            m["c_nav"] = np.ascontiguousarray(np.asarray(inp["cache_na_v"], f32)[i])
            m["c_gqk"] = np.ascontiguousarray(np.asarray(inp["cache_gqa_k"], f32)[i])
            m["c_gqv"] = np.ascontiguousarray(np.asarray(inp["cache_gqa_v"], f32)[i])
            m["s_gla"] = np.ascontiguousarray(np.asarray(inp["state_gla"], f32)[i])
            m["s_gdn"] = np.ascontiguousarray(np.asarray(inp["state_gdn"], f32)[i])
        maps.append(m)
    return maps


def run(cfg, inp, n_cores, trace=False):
    nc = build(cfg)
    maps = make_in_maps(cfg, inp, n_cores)
    res = run_bass_kernel_spmd(nc, maps, core_ids=list(range(n_cores)), trace=trace)
    r = res.results
    run.last = r
    TP = cfg.TP
    yp = np.stack([r[i]["y"][k * TP:(k + 1) * TP] for i in range(n_cores) for k in range(2)], axis=0)
    ys = np.stack([r[i]["y"][2 * TP:] for i in range(n_cores)], axis=0)
    outs = [yp, ys]
    if cfg.mixers:
        for nm in ("o_nak", "o_nav", "o_gqk", "o_gqv", "o_gla", "o_gdn"):
            outs.append(np.concatenate([r[i][nm] for i in range(n_cores)], axis=0))
    return tuple(outs), res


def kernel(**inputs):
    cfg = Cfg()
    outs, _ = run(cfg, inputs, 8)
    return outs
```
